# Optimizing a Trainium2 kernel written in Bass

```python
import math
import jax, jax.numpy as jnp
from jax import lax
import numpy as np

D_MODEL = 1024
BATCH = 4
SEQ = 4096
DEPTH = 2
DEC_BATCH = 32
DEC_SEQ = 8
PAST_LEN = 16384
PAGE_SIZE = 128

N_META = 16
MIX_WIDTH = D_MODEL
SB_HEADS = 8
SB_WIDTH = MIX_WIDTH // 2
SB_HEAD_DIM = SB_WIDTH // SB_HEADS
SB_BIAS_INIT = -6.0
Q_BLOCK = 128
POOL_WINDOWS = (2, 4, 8, 16)
POOL_GROUPS = len(POOL_WINDOWS)
POOL_WIDTH = MIX_WIDTH // 2
POOL_GROUP_DIM = POOL_WIDTH // POOL_GROUPS
POOL_CTX = max(POOL_WINDOWS) - 1
LRU_WIDTH = D_MODEL
LRU_BLOCKS = 8
LRU_BLOCK_DIM = LRU_WIDTH // LRU_BLOCKS
LRU_CONV = 4
LRU_C = 8.0
FFN_DIM = 2816
FFN_CONV = 3
LN_EPS = 1e-5
DN_ALPHA = (2.0 * DEPTH) ** 0.25
DN_BETA = (8.0 * DEPTH) ** -0.25
N_SB_LAYERS = (DEPTH + 1) // 2
N_LRU_LAYERS = DEPTH // 2

kernel_name = 'hybrid_stickbreak_pool_rglru_convffn_step'


def _layer_norm(x, g, b):
    xf = x.astype(jnp.float32)
    mu = jnp.mean(xf, axis=-1, keepdims=True)
    xc = xf - mu
    var = jnp.mean(xc * xc, axis=-1, keepdims=True)
    y = xc * lax.rsqrt(var + LN_EPS) * g.astype(jnp.float32) + b.astype(jnp.float32)
    return y.astype(x.dtype)


def _causal_dwconv(u, ctx, w, b):
    width = w.shape[0]
    ext = jnp.concatenate([ctx.astype(u.dtype), u], axis=1)
    y = lax.conv_general_dilated(ext, w[:, None, :].astype(u.dtype), (1,), 'VALID',
                                 dimension_numbers=('NWC', 'WIO', 'NWC'),
                                 feature_group_count=u.shape[-1])
    return y + b.astype(u.dtype), ext[:, ext.shape[1] - (width - 1):]


def _sb_weights(z, mask):
    log_keep = jnp.where(mask, jax.nn.log_sigmoid(-z), 0.0)
    after = lax.cumsum(log_keep, axis=z.ndim - 1, reverse=True) - log_keep
    return jnp.where(mask, jnp.exp(jax.nn.log_sigmoid(z) + after), 0.0)


def _sb_prompt(q, k, v, bias):
    bsz, length, heads, hd = q.shape
    scale = hd ** -0.5
    hb = bias.astype(jnp.float32)[:, None, None]
    pos = jnp.arange(length)
    zm = jnp.einsum('bqhd,bkhd->bhqk', q[:, :N_META], k[:, :N_META],
                    preferred_element_type=jnp.float32) * scale + hb
    am = _sb_weights(zm, pos[None, :N_META] < pos[:N_META, None])
    o_meta = jnp.einsum('bhqk,bkhd->bqhd', am.astype(v.dtype), v[:, :N_META])
    n_real = length - N_META
    nb = n_real // Q_BLOCK
    q_blocks = q[:, N_META:].reshape(bsz, nb, Q_BLOCK, heads, hd).transpose(1, 0, 2, 3, 4)
    qpos_blocks = (N_META + jnp.arange(n_real)).reshape(nb, Q_BLOCK)

    def block(args):
        qb, qp = args
        zb = jnp.einsum('bqhd,bkhd->bhqk', qb, k, preferred_element_type=jnp.float32) * scale + hb
        ab = _sb_weights(zb, pos[None, :] < qp[:, None])
        return jnp.einsum('bhqk,bkhd->bqhd', ab.astype(v.dtype), v)

    o_real = lax.map(block, (q_blocks, qpos_blocks))
    o_real = o_real.transpose(1, 0, 2, 3, 4).reshape(bsz, n_real, heads, hd)
    return jnp.concatenate([o_meta, o_real], axis=1)


def _sb_sample(q, k, v, cache_k, cache_v, page_table, bias):
    bsz, t, heads, hd = q.shape
    scale = hd ** -0.5
    hb = bias.astype(jnp.float32)[:, None, None]
    past = page_table.shape[1] * cache_k.shape[1]
    k_past = cache_k[page_table].reshape(bsz, past, heads, hd)
    v_past = cache_v[page_table].reshape(bsz, past, heads, hd)
    z = jnp.concatenate([
        jnp.einsum('bqhd,bkhd->bhqk', q, k_past.astype(q.dtype), preferred_element_type=jnp.float32),
        jnp.einsum('bqhd,bkhd->bhqk', q, k, preferred_element_type=jnp.float32)], axis=-1) * scale + hb
    qpos = past + jnp.arange(t)
    kpos = jnp.arange(past + t)
    a = _sb_weights(z, kpos[None, :] < qpos[:, None]).astype(v.dtype)
    return (jnp.einsum('bhqk,bkhd->bqhd', a[..., :past], v_past.astype(v.dtype))
            + jnp.einsum('bhqk,bkhd->bqhd', a[..., past:], v))


def _multiscale_pool(u, ctx, pos0, w_grp, scale):
    bsz, t, c = u.shape
    ext = jnp.concatenate([ctx.astype(u.dtype), u], axis=1)
    ext32 = ext.astype(jnp.float32)
    cs = jnp.concatenate([jnp.zeros((bsz, 1, c), jnp.float32), jnp.cumsum(ext32, axis=1)], axis=1)
    pos = pos0 + jnp.arange(t)
    means = []
    for g, w in enumerate(POOL_WINDOWS):
        sl = slice(g * POOL_GROUP_DIM, (g + 1) * POOL_GROUP_DIM)
        hi = cs[:, POOL_CTX + 1:POOL_CTX + 1 + t, sl]
        lo = cs[:, POOL_CTX + 1 - w:POOL_CTX + 1 - w + t, sl]
        cnt = jnp.minimum(pos + 1, w).astype(jnp.float32)[None, :, None]
        means.append((hi - lo) / cnt)
    d = (jnp.concatenate(means, axis=-1) - ext32[:, POOL_CTX:]).reshape(bsz, t, POOL_GROUPS, POOL_GROUP_DIM)
    mixed = jnp.einsum('btgc,gcd->btgd', d, w_grp.astype(jnp.float32)).reshape(bsz, t, c)
    return (mixed * scale.astype(jnp.float32)).astype(u.dtype), ext[:, ext.shape[1] - POOL_CTX:]


def _rg_lru(xc, h0, w_a, b_a, w_x, b_x, lam):
    bsz, t, r = xc.shape
    f32 = jnp.float32
    xf = xc.astype(f32)
    xb = xf.reshape(bsz, t, LRU_BLOCKS, LRU_BLOCK_DIM)
    gate_r = jax.nn.sigmoid(jnp.einsum('btnc,ncd->btnd', xb, w_a.astype(f32))
                            + b_a.astype(f32).reshape(LRU_BLOCKS, LRU_BLOCK_DIM)).reshape(bsz, t, r)
    gate_i = jax.nn.sigmoid(jnp.einsum('btnc,ncd->btnd', xb, w_x.astype(f32))
                            + b_x.astype(f32).reshape(LRU_BLOCKS, LRU_BLOCK_DIM)).reshape(bsz, t, r)
    log_a = -LRU_C * gate_r * jax.nn.softplus(-lam.astype(f32))
    a = jnp.exp(log_a)
    b = jnp.sqrt(-jnp.expm1(2.0 * log_a)) * gate_i * xf
    b = b.at[:, 0].add(a[:, 0] * h0.astype(f32))

    def combine(left, right):
        a1, b1 = left
        a2, b2 = right
        return a1 * a2, a2 * b1 + b2

    _, h = lax.associative_scan(combine, (a, b), axis=1)
    return h.astype(xc.dtype), h[:, -1].astype(xc.dtype)


def _conv_ffn(x, ctx, w_up, conv_w, conv_b, w_down):
    g, v = jnp.split(x @ w_up, 2, axis=-1)
    gc, new_ctx = _causal_dwconv(g, ctx, conv_w, conv_b)
    return (jax.nn.gelu(gc) * v) @ w_down, new_ctx


def _trunk(x, pos0, attend, pool_ctx, lru_conv_ctx, lru_h0, ffn_ctx, p):
    bsz, t, _ = x.shape
    new_k, new_v, new_pool, new_conv, new_h, new_ffn = [], [], [], [], [], []
    for layer in range(DEPTH):
        j = layer // 2
        if layer % 2 == 0:
            proj = x @ p['sb_w_in'][j]
            q, k, v, u = jnp.split(proj, [SB_WIDTH, 2 * SB_WIDTH, 3 * SB_WIDTH], axis=-1)
            q = q.reshape(bsz, t, SB_HEADS, SB_HEAD_DIM)
            k = k.reshape(bsz, t, SB_HEADS, SB_HEAD_DIM)
            v = v.reshape(bsz, t, SB_HEADS, SB_HEAD_DIM)
            o_a = attend(j, q, k, v, p['sb_logit_bias'][j]).reshape(bsz, t, SB_WIDTH)
            o_b, pc = _multiscale_pool(u, pool_ctx[j], pos0, p['pool_w'][j], p['pool_scale'][j])
            out = jnp.concatenate([o_a, o_b], axis=-1) @ p['sb_w_out'][j]
            new_k.append(k)
            new_v.append(v)
            new_pool.append(pc)
        else:
            gate, rec = jnp.split(x @ p['lru_w_in'][j], 2, axis=-1)
            xc, cc = _causal_dwconv(rec, lru_conv_ctx[j], p['lru_conv_w'][j], p['lru_conv_b'][j])
            h, hl = _rg_lru(xc, lru_h0[j], p['lru_w_a'][j], p['lru_b_a'][j], p['lru_w_x'][j],
                            p['lru_b_x'][j], p['lru_lambda'][j])
            out = (jax.nn.gelu(gate) * h) @ p['lru_w_out'][j]
            new_conv.append(cc)
            new_h.append(hl)
        x = _layer_norm(DN_ALPHA * x + out, p['ln_g'][layer, 0], p['ln_b'][layer, 0])
        f, fc = _conv_ffn(x, ffn_ctx[layer], p['ffn_w_up'][layer], p['ffn_conv_w'][layer],
                          p['ffn_conv_b'][layer], p['ffn_w_down'][layer])
        new_ffn.append(fc)
        x = _layer_norm(DN_ALPHA * x + f, p['ln_g'][layer, 1], p['ln_b'][layer, 1])
    return (x, jnp.stack(new_k), jnp.stack(new_v), jnp.stack(new_pool), jnp.stack(new_conv),
            jnp.stack(new_h), jnp.stack(new_ffn))


def setup_inputs(seed: int = 0) -> dict:
    key = jax.random.key(seed)
    ks = jax.random.split(key, 30)
    n_pages = PAST_LEN // PAGE_SIZE
    used = DEC_BATCH * n_pages
    n_phys = used + max(1, used // 4)
    nrm = jax.random.normal
    f32 = jnp.float32
    page_table = jax.random.permutation(ks[4], n_phys)[:used].reshape(DEC_BATCH, n_pages).astype(jnp.int32)
    u = jax.random.uniform(ks[21], (N_LRU_LAYERS, LRU_WIDTH), f32, 0.9, 0.999)
    s = u ** (1.0 / LRU_C)
    lru_lambda = jnp.log(s) - jnp.log1p(-s)
    return {
        'x_prompt': nrm(ks[0], (BATCH, SEQ, D_MODEL), f32),
        'x_sample': nrm(ks[1], (DEC_BATCH, DEC_SEQ, D_MODEL), f32),
        'cache_sb_k': nrm(ks[2], (N_SB_LAYERS, n_phys, PAGE_SIZE, SB_HEADS, SB_HEAD_DIM), f32),
        'cache_sb_v': nrm(ks[3], (N_SB_LAYERS, n_phys, PAGE_SIZE, SB_HEADS, SB_HEAD_DIM), f32),
        'page_table': page_table,
        'state_pool': nrm(ks[5], (N_SB_LAYERS, DEC_BATCH, POOL_CTX, POOL_WIDTH), f32),
        'state_lru_conv': nrm(ks[6], (N_LRU_LAYERS, DEC_BATCH, LRU_CONV - 1, LRU_WIDTH), f32),
        'state_lru_h': 0.5 * nrm(ks[7], (N_LRU_LAYERS, DEC_BATCH, LRU_WIDTH), f32),
        'state_ffn_conv': nrm(ks[8], (DEPTH, DEC_BATCH, FFN_CONV - 1, FFN_DIM), f32),
        'meta_tokens': nrm(ks[9], (N_META, D_MODEL), f32),
        'sb_w_in': nrm(ks[10], (N_SB_LAYERS, D_MODEL, 3 * SB_WIDTH + POOL_WIDTH), f32) * D_MODEL ** -0.5,
        'sb_logit_bias': SB_BIAS_INIT + 0.1 * nrm(ks[29], (N_SB_LAYERS, SB_HEADS), f32),
        'sb_w_out': nrm(ks[11], (N_SB_LAYERS, SB_WIDTH + POOL_WIDTH, D_MODEL), f32)
                    * ((SB_WIDTH + POOL_WIDTH) ** -0.5 * DN_BETA),
        'pool_w': nrm(ks[12], (N_SB_LAYERS, POOL_GROUPS, POOL_GROUP_DIM, POOL_GROUP_DIM), f32)
                  * POOL_GROUP_DIM ** -0.5,
        'pool_scale': 1.0 + 0.1 * nrm(ks[13], (N_SB_LAYERS, POOL_WIDTH), f32),
        'lru_w_in': nrm(ks[14], (N_LRU_LAYERS, D_MODEL, 2 * LRU_WIDTH), f32) * D_MODEL ** -0.5,
        'lru_conv_w': nrm(ks[15], (N_LRU_LAYERS, LRU_CONV, LRU_WIDTH), f32) * LRU_CONV ** -0.5,
        'lru_conv_b': 0.01 * nrm(ks[16], (N_LRU_LAYERS, LRU_WIDTH), f32),
        'lru_w_a': nrm(ks[17], (N_LRU_LAYERS, LRU_BLOCKS, LRU_BLOCK_DIM, LRU_BLOCK_DIM), f32)
                   * LRU_BLOCK_DIM ** -0.5,
        'lru_b_a': 0.01 * nrm(ks[18], (N_LRU_LAYERS, LRU_WIDTH), f32),
        'lru_w_x': nrm(ks[19], (N_LRU_LAYERS, LRU_BLOCKS, LRU_BLOCK_DIM, LRU_BLOCK_DIM), f32)
                   * LRU_BLOCK_DIM ** -0.5,
        'lru_b_x': 0.01 * nrm(ks[20], (N_LRU_LAYERS, LRU_WIDTH), f32),
        'lru_lambda': lru_lambda,
        'lru_w_out': nrm(ks[22], (N_LRU_LAYERS, LRU_WIDTH, D_MODEL), f32) * (LRU_WIDTH ** -0.5 * DN_BETA),
        'ffn_w_up': nrm(ks[23], (DEPTH, D_MODEL, 2 * FFN_DIM), f32) * D_MODEL ** -0.5,
        'ffn_conv_w': nrm(ks[24], (DEPTH, FFN_CONV, FFN_DIM), f32) * FFN_CONV ** -0.5,
        'ffn_conv_b': 0.01 * nrm(ks[25], (DEPTH, FFN_DIM), f32),
        'ffn_w_down': nrm(ks[26], (DEPTH, FFN_DIM, D_MODEL), f32) * (FFN_DIM ** -0.5 * DN_BETA),
        'ln_g': 1.0 + 0.05 * nrm(ks[27], (DEPTH, 2, D_MODEL), f32),
        'ln_b': 0.02 * nrm(ks[28], (DEPTH, 2, D_MODEL), f32),
    }


def reference(x_prompt, x_sample, cache_sb_k, cache_sb_v, page_table, state_pool, state_lru_conv,
              state_lru_h, state_ffn_conv, meta_tokens, sb_w_in, sb_logit_bias, sb_w_out, pool_w, pool_scale,
              lru_w_in, lru_conv_w, lru_conv_b, lru_w_a, lru_b_a, lru_w_x, lru_b_x, lru_lambda,
              lru_w_out, ffn_w_up, ffn_conv_w, ffn_conv_b, ffn_w_down, ln_g, ln_b):
    p = dict(sb_w_in=sb_w_in, sb_logit_bias=sb_logit_bias, sb_w_out=sb_w_out, pool_w=pool_w,
             pool_scale=pool_scale, lru_w_in=lru_w_in, lru_conv_w=lru_conv_w, lru_conv_b=lru_conv_b,
             lru_w_a=lru_w_a, lru_b_a=lru_b_a, lru_w_x=lru_w_x, lru_b_x=lru_b_x, lru_lambda=lru_lambda,
             lru_w_out=lru_w_out, ffn_w_up=ffn_w_up, ffn_conv_w=ffn_conv_w, ffn_conv_b=ffn_conv_b,
             ffn_w_down=ffn_w_down, ln_g=ln_g, ln_b=ln_b)
    dt = x_prompt.dtype
    bp = x_prompt.shape[0]
    xp = jnp.concatenate([jnp.broadcast_to(meta_tokens[None].astype(dt), (bp, N_META, D_MODEL)), x_prompt], axis=1)
    (hp, k_prompt, v_prompt, pool_prompt, lru_conv_prompt, lru_h_prompt, ffn_conv_prompt) = _trunk(
        xp, 0, lambda j, q, k, v, bias: _sb_prompt(q, k, v, bias),
        jnp.zeros((N_SB_LAYERS, bp, POOL_CTX, POOL_WIDTH), dt),
        jnp.zeros((N_LRU_LAYERS, bp, LRU_CONV - 1, LRU_WIDTH), dt),
        jnp.zeros((N_LRU_LAYERS, bp, LRU_WIDTH), dt),
        jnp.zeros((DEPTH, bp, FFN_CONV - 1, FFN_DIM), dt), p)
    y_prompt = hp[:, N_META:]
    past_len = page_table.shape[1] * cache_sb_k.shape[2]
    (y_sample, k_sample, v_sample, pool_sample, lru_conv_sample, lru_h_sample, ffn_conv_sample) = _trunk(
        x_sample, past_len,
        lambda j, q, k, v, bias: _sb_sample(q, k, v, cache_sb_k[j], cache_sb_v[j], page_table, bias),
        state_pool, state_lru_conv, state_lru_h, state_ffn_conv, p)
    return (y_prompt, y_sample, k_prompt, v_prompt, k_sample, v_sample, pool_prompt, pool_sample,
            lru_conv_prompt, lru_conv_sample, lru_h_prompt, lru_h_sample, ffn_conv_prompt, ffn_conv_sample)
```

```python
import numpy as np
from contextlib import ExitStack
import concourse.bass as bass
import concourse.mybir as mybir
from concourse.bass_utils import run_bass_kernel_spmd

F32 = mybir.dt.float32
BF16 = mybir.dt.bfloat16
I32 = mybir.dt.int32
U32 = mybir.dt.uint32
AF = mybir.ActivationFunctionType
ALU = mybir.AluOpType

D = 1024
SEQ = 4096
NMETA = 16
L = SEQ + NMETA
NT = 512
NG = 8
NBLK = 33
FFN = 2816
NFC = 22
PAGES = 128
NPHYS = 5120
DN_ALPHA = (2.0 * 2) ** 0.25
LN_EPS = 1e-5
NEG = -30000.0
LIMIT = 0

O_LNG, O_LNB, O_PSC, O_LCW, O_LCB, O_LBA, O_LBX, O_LAM, O_FCW, O_FCB, O_HB, VR = (
    0, 32, 64, 68, 100, 108, 116, 124, 132, 264, 308, 316)

ENGS = ("pe", "act", "dve", "pool", "sp")


class Buf:
    __slots__ = ("w", "r", "excl")

    def __init__(self, excl=False):
        self.w = None
        self.r = {}
        self.excl = excl


class Sched:
    K = 8

    def __init__(self, nc, es):
        self.nc = nc
        self.prog = {e: [] for e in ENGS}
        self.sem = {e: es.enter_context(nc.semaphore("s_" + e)) for e in ENGS}
        self.cnt = {e: 0 for e in ENGS}
        self.seen = {e: {} for e in ENGS}
        self.ring = {q: [es.enter_context(nc.semaphore("r_%s_%d" % (q, i))) for i in range(self.K)]
                     for q in ("sp", "pool")}
        self.rcnt = {q: 0 for q in ("sp", "pool")}
        self.seq = 0
        self.limit = LIMIT
        self.marks = []

    def _waits(self, e, toks):
        need = {}
        for t in toks:
            if t is None:
                continue
            sem, val, te = t
            if te == "pe" and e == "pe":
                continue
            k = id(sem)
            if self.seen[e].get(k, 0) >= val:
                continue
            if k not in need or need[k][1] < val:
                need[k] = (sem, val)
        for k, (sem, val) in need.items():
            self.seen[e][k] = val
        return list(need.values())

    @staticmethod
    def _deps(reads, writes, e=None):
        toks = [b.w for b in reads]
        for b in reads:
            if b.excl:
                toks.extend(t for t in b.r.values() if t[2] != e)
        for b in writes:
            toks.append(b.w)
            toks.extend(b.r.values())
        return toks

    @staticmethod
    def _update(tok, reads, writes):
        for b in writes:
            b.w = tok
            b.r = {}
        k = id(tok[0])
        for b in reads:
            o = b.r.get(k)
            if o is None or o[1] < tok[1]:
                b.r[k] = tok

    def mark(self, label):
        self.marks.append((label, self.seq, {e: len(self.prog[e]) for e in ENGS}))

    def op(self, e, fn, reads=(), writes=(), signal=True):
        self.seq += 1
        if self.limit and self.seq > self.limit:
            return
        waits = self._waits(e, self._deps(reads, writes, e))
        if signal:
            self.cnt[e] += 1
            tok = (self.sem[e], self.cnt[e], e)
            inc = (self.sem[e], 1)
        else:
            tok = (self.sem[e], self.cnt[e] + 1, e)
            inc = None
        self.prog[e].append((waits, fn, inc))
        self._update(tok, reads, writes)

    def dma(self, q, fn, reads=(), writes=()):
        self.seq += 1
        if self.limit and self.seq > self.limit:
            return
        i = self.rcnt[q] % self.K
        n = self.rcnt[q] // self.K
        self.rcnt[q] += 1
        sem = self.ring[q][i]
        toks = self._deps(reads, writes, q)
        if n > 0:
            toks.append((sem, 16 * n, "dma"))
        waits = self._waits(q, toks)
        tok = (sem, 16 * (n + 1), "dma")
        self.prog[q].append((waits, fn, (sem, 16)))
        self._update(tok, reads, writes)

    def finish(self):
        waits = []
        for q in ("sp", "pool"):
            for i in range(self.K):
                n = (self.rcnt[q] - i + self.K - 1) // self.K
                if n > 0:
                    waits.append((self.ring[q][i], 16 * n))
        self.prog["sp"].append((waits, None, None))

    def emit(self, block):
        def mk(e):
            def body(eng):
                for waits, fn, inc in self.prog[e]:
                    for sem, val in waits:
                        eng.wait_ge(sem, val)
                    if fn is None:
                        continue
                    ins = fn(eng)
                    if inc is not None:
                        ins.then_inc(*inc)
            return body
        block.tensor(mk("pe"))
        block.scalar(mk("act"))
        block.vector(mk("dve"))
        block.gpsimd(mk("pool"))
        block.sync(mk("sp"))


class Rot:
    def __init__(self, nc, es, name, n, width, dt):
        self.t = es.enter_context(nc.sbuf_tensor(name, [128, n, width], dt))
        self.b = [Buf() for _ in range(n)]
        self.n = n
        self.i = 0

    def next(self):
        i = self.i
        self.i = (i + 1) % self.n
        return self.t[:, i, :], self.b[i]


def build_program():
    nc = bass.Bass("TRN2", target_bir_lowering=False)
    es = ExitStack()
    with es:
        def din(name, shape, dt=F32):
            return nc.dram_tensor(name, shape, dt, kind="ExternalInput").ap()

        def dout(name, shape):
            return nc.dram_tensor(name, shape, F32, kind="ExternalOutput").ap()

        xp = din("xp", [SEQ, D]); meta = din("meta", [NMETA, D]); xs = din("xs", [32, D])
        ck = din("ck", [NPHYS * 128, 512]); cv = din("cv", [NPHYS * 128, 512])
        pt = din("pt", [4, PAGES], I32)
        spool = din("spool", [60, 512]); slc = din("slc", [12, D]); slh = din("slh", [4, D])
        sfc = din("sfc", [2, 8, FFN])
        w_in0 = din("w_in0", [D, 2048]); w_out0 = din("w_out0", [D, D]); poolw = din("poolw", [512, 128])
        lw_in = din("lw_in", [D, 2048]); lwa = din("lwa", [D, 128]); lwx = din("lwx", [D, 128])
        lw_out = din("lw_out", [D, D])
        fup = [din("fup0", [D, 2 * FFN]), din("fup1", [D, 2 * FFN])]
        fdn = [din("fdn0", [FFN, D]), din("fdn1", [FFN, D])]
        vec = din("vec", [128, VR]); hbr = din("hbr", [1, 384])

        yp = dout("yp", [SEQ, D]); ys = dout("ys", [32, D])
        kp = dout("kp", [L, 512]); vp = dout("vp", [L, 512]); ks = dout("ks", [32, 512]); vs = dout("vs", [32, 512])
        poolp = dout("poolp", [15, 512]); pools = dout("pools", [60, 512])
        lcp = dout("lcp", [3, D]); lcs = dout("lcs", [12, D]); lhp = dout("lhp", [1, D]); lhs = dout("lhs", [4, D])
        fcp = dout("fcp", [2, 2, FFN]); fcs = dout("fcs", [2, 8, FFN])

        S = Sched(nc, es)

        def sb(name, shape, dt=F32):
            return es.enter_context(nc.sbuf_tensor(name, shape, dt))

        identF = sb("identF", [128, 128]); identB = sb("identB", [128, 128], BF16)
        negtri = sb("negtri", [128, 128], BF16); negones = sb("negones", [128, 128], BF16)
        onesB = sb("onesB", [128, 128], BF16); meanm = sb("meanm", [128, 128], BF16)
        zerosB = sb("zerosB", [128, 128], BF16); maskM = sb("maskM", [128, 128], BF16)
        maskQ8 = sb("maskQ8", [128, 256], BF16); maskQ16 = sb("maskQ16", [128, 128], BF16)
        onesF = sb("onesF", [128, 128])
        CB = Buf()
        VEC = sb("VEC", [128, VR]); VB = Buf()
        SP8 = sb("SP8", [128, 8]); SPT = sb("SPT", [128, 8]); SP4 = sb("SP4", [128, 8])
        RC = sb("RC", [128, 4, 16]); RCI = sb("RCI", [128, 16], I32)
        HBF = sb("HBF", [1, 384]); HBH = sb("HBH", [1, 384], BF16); HBL = sb("HBL", [1, 384], BF16)
        HBT = sb("HBT", [1, 384])
        PW = sb("PW", [128, 4, 128], BF16); PWB = Buf()
        LWA = sb("LWA", [128, 8, 128], BF16); LWX = sb("LWX", [128, 8, 128], BF16); LWB = Buf()

        xin = sb("xin", [128, 8, 512]); xinb = [Buf() for _ in range(8)]
        xT = sb("xT", [128, 8, NT]); xTb = sb("xTb", [128, 8, NT], BF16)
        xTB = [Buf() for _ in range(8)]; xTbB = [Buf() for _ in range(8)]
        KT = sb("KT", [128, 4, L], BF16); KTB = [Buf() for _ in range(4)]
        V = sb("V", [128, NBLK, 512], BF16); VBl = [Buf() for _ in range(NBLK)]
        qT = sb("qT", [128, 4, NT], BF16); qTB = [Buf() for _ in range(4)]
        KTS = sb("KTS", [128, 4, 32], BF16); KTSB = Buf()
        QZ = sb("QZ", [128, 8, 48], BF16); QZB = Buf()
        VSN = sb("VSN", [8, 4, 512], BF16); VSNB = [Buf() for _ in range(4)]
        NW = 10
        WR = sb("WR", [128, NW, 512], BF16); WRB = [Buf() for _ in range(NW)]
        wr_i = [0]
        gg = sb("gg", [128, NFC, NT], BF16); ggB = [Buf() for _ in range(NFC)]
        RF = Rot(nc, es, "RF", 9, 528, F32)
        RB = Rot(nc, es, "RB", 8, 512, BF16)
        ACC = sb("ACC", [128, 2, 512]); ACCB = sb("ACCB", [128, 2, 512], BF16)
        ACCb = [Buf(), Buf()]; ACCBb = [Buf(), Buf()]
        MEAN = sb("MEAN", [128, NT]); RSTD = sb("RSTD", [128, NT]); MR = sb("MR", [128, NT])
        MEANb, RSTDb, MRb = Buf(), Buf(), Buf()
        UHALO = sb("UHALO", [128, 4, 15]); RHALO = sb("RHALO", [128, 8, 3]); GHALO = sb("GHALO", [128, 2 * NFC, 2])
        UHb = [Buf() for _ in range(4)]; RHb = [Buf() for _ in range(8)]; GHb = [Buf() for _ in range(2 * NFC)]
        HST = sb("HST", [128, 8]); HSTb = [Buf() for _ in range(8)]
        HL = sb("HL", [128, 8, 5]); HLb = Buf()
        SU = sb("SU", [128, 4, 60]); SR = sb("SR", [128, 8, 12]); SG = sb("SG", [128, 2 * NFC, 8]); SH = sb("SH", [128, 8, 4])
        SSb = Buf()
        PTB = sb("PTB", [128, 1, 512], I32); PI = sb("PI", [128, 1], I32); PIF = sb("PIF", [128, 1])
        IDX = sb("IDX", [128, 512], U32); IDXb = Buf()

        ps = [es.enter_context(nc.psum_tensor("ps%d" % i, [128, 512], F32)) for i in range(8)]
        psb = [Buf(excl=True) for _ in range(8)]

        def cat(i):
            return gg[:, 8 + i, :]
        catB = ggB[8:16]

        def MM(out, lhsT, rhs, start, stop, reads, writes, signal=True):
            S.op("pe", lambda e: e.matmul(out, lhsT=lhsT, rhs=rhs, start=start, stop=stop), reads, writes, signal)

        def TR(out, in_, ident, reads, writes):
            S.op("pe", lambda e: e.transpose(out, in_, ident), reads, writes)

        def ACTF(out, in_, func, reads, writes, bias=None, scale=None):
            kw = {}
            if bias is not None:
                kw["bias"] = bias
            if scale is not None:
                kw["scale"] = scale
            S.op("act", lambda e: e.activation(out, in_, func, **kw), reads, writes)

        def CP(eng, out, in_, reads, writes):
            if eng == "act":
                S.op("act", lambda e: e.activation(out, in_, AF.Copy), reads, writes)
            else:
                S.op(eng, lambda e: e.tensor_copy(out=out, in_=in_), reads, writes)

        def TS(out, in0, s1, s2, op0, op1, reads, writes, eng="dve"):
            if op1 is None:
                S.op(eng, lambda e: e.tensor_scalar(out=out, in0=in0, scalar1=s1, scalar2=None, op0=op0), reads, writes)
            else:
                S.op(eng, lambda e: e.tensor_scalar(out=out, in0=in0, scalar1=s1, scalar2=s2, op0=op0, op1=op1), reads, writes)

        def STT(out, in0, scalar, in1, op0, op1, reads, writes):
            S.op("dve", lambda e: e.scalar_tensor_tensor(out=out, in0=in0, scalar=scalar, in1=in1, op0=op0, op1=op1), reads, writes)

        def TT(out, in0, in1, op, reads, writes, eng="dve"):
            S.op(eng, lambda e: e.tensor_tensor(out=out, in0=in0, in1=in1, op=op), reads, writes)

        def MEMSET(eng, ap, val, writes):
            S.op(eng, lambda e: e.memset(ap, val), (), writes)

        def DMA(q, out, in_, reads, writes):
            S.dma(q, lambda e: e.dma_start(out=out, in_=in_), reads, writes)

        def vcol(off):
            return VEC[:, off:off + 1]

        MEMSET("pool", onesF[:], 1.0, [CB]); MEMSET("pool", onesB[:], 1.0, [CB])
        MEMSET("pool", negones[:], -1.0, [CB]); MEMSET("pool", meanm[:], 1.0 / 1024.0, [CB])
        MEMSET("pool", zerosB[:], 0.0, [CB])

        def ASEL(out, in_, pattern, op, fill, cm):
            S.op("pool", lambda e: e.affine_select(out=out, in_=in_, pattern=pattern, compare_op=op, fill=fill,
                                                   base=0, channel_multiplier=cm), [CB], [CB])
        ASEL(identF[:], onesF[:], [[-1, 128]], ALU.is_equal, 0.0, 1)
        ASEL(identB[:], onesB[:], [[-1, 128]], ALU.is_equal, 0.0, 1)
        ASEL(negtri[:], negones[:], [[-1, 128]], ALU.is_ge, 0.0, 1)
        ASEL(maskM[:], zerosB[:], [[1, 128]], ALU.is_gt, NEG, -1)
        ASEL(maskQ8[:, 0:128], zerosB[:], [[0, 16], [1, 8]], ALU.is_gt, NEG, -1)
        ASEL(maskQ8[:, 128:256], zerosB[:], [[0, 16], [1, 8]], ALU.is_gt, NEG, -1)
        ASEL(maskQ16[:], zerosB[:], [[0, 8], [1, 16]], ALU.is_gt, NEG, -1)

        DMA("sp", VEC[:], vec, [], [VB])
        DMA("sp", HBF[:], hbr, [], [VB])
        CP("dve", HBH[:], HBF[:], [VB], [VB])
        CP("dve", HBT[:], HBH[:], [VB], [VB])
        TT(HBT[:], HBF[:], HBT[:], ALU.subtract, [VB], [VB])
        CP("dve", HBL[:], HBT[:], [VB], [VB])
        ACTF(SPT[:], VEC[:, O_LAM:O_LAM + 8], AF.Exp, [VB], [VB], scale=-1.0)
        ACTF(SPT[:], SPT[:], AF.Ln, [VB], [VB], bias=1.0)
        TS(SP8[:], SPT[:], -8.0, None, ALU.mult, None, [VB], [VB])
        TS(SP4[:], SPT[:], -4.0, None, ALU.mult, None, [VB], [VB])
        S.op("pool", lambda e: e.iota(RCI[:], pattern=[[1, 16]], base=1, channel_multiplier=0), [], [VB])
        for c in range(4):
            CP("dve", RC[:, c, :], RCI[:], [VB], [VB])
            TS(RC[:, c, :], RC[:, c, :], float(2 << c), None, ALU.min, None, [VB], [VB])
            S.op("dve", (lambda cc: (lambda e: e.reciprocal(out=RC[:, cc, :], in_=RC[:, cc, :])))(c), [VB], [VB])
        DMA("pool", PW[:], poolw.rearrange("(g c) d -> c g d", c=128), [], [PWB])
        DMA("pool", LWA[:], lwa.rearrange("(n c) d -> c n d", c=128), [], [LWB])
        DMA("pool", LWX[:], lwx.rearrange("(n c) d -> c n d", c=128), [], [LWB])
        MEMSET("dve", UHALO[:], 0.0, UHb); MEMSET("dve", RHALO[:], 0.0, RHb)
        MEMSET("dve", GHALO[:], 0.0, GHb); MEMSET("dve", HST[:], 0.0, HSTb)
        DMA("sp", PTB[:], pt.rearrange("(o s) j -> o (s j)", o=1).partition_broadcast(128), [], [IDXb])
        S.op("pool", lambda e: e.iota(PI[:], pattern=[[0, 1]], base=0, channel_multiplier=1), [], [IDXb])
        CP("dve", PIF[:], PI[:], [IDXb], [IDXb])
        TS(IDX[:], PTB[:, 0, :], 128.0, PIF[:, 0:1], ALU.mult, ALU.add, [IDXb], [IDXb])

        def load_units(W, K, c0, width=512):
            return {"W": W, "K": K, "c0": c0, "width": width, "slots": {}}

        def unit_slot(u, kc):
            if kc not in u["slots"]:
                s = wr_i[0]
                wr_i[0] = (s + 1) % NW
                DMA("pool", WR[:, s, :u["width"]], u["W"][128 * kc:128 * kc + 128, u["c0"]:u["c0"] + u["width"]], [], [WRB[s]])
                u["slots"][kc] = s
            return u["slots"][kc]

        def fm_pass(units, rhs, n, banks, width=512):
            K = units["K"]
            noc = width // 128
            for kc in range(K):
                s = unit_slot(units, kc)
                rap, rb = rhs(kc)
                for oc in range(noc):
                    MM(ps[banks[oc]][:, :n], WR[:, s, 128 * oc:128 * oc + 128], rap[:, :n], kc == 0, kc == K - 1,
                       [WRB[s], rb], [psb[banks[oc]]], signal=(oc == noc - 1))

        def tm_pass(units, segs, banks, width=512):
            K = units["K"]
            for kc in range(K):
                s = unit_slot(units, kc)
                for si, (c0, m) in enumerate(segs):
                    MM(ps[banks[si]][:m, :width], xTb[:, kc, c0:c0 + m], WR[:, s, :width], kc == 0, kc == K - 1,
                       [WRB[s], xTbB[kc]], [psb[banks[si]]], signal=(si == len(segs) - 1))

        def xrhs(kc):
            return xTb[:, kc, :], xTbB[kc]

        def residual_ln(W, K, rhs, n, layer, which):
            for half in range(2):
                banks = [0, 1, 2, 3] if half == 0 else [4, 5, 6, 7]
                units = load_units(W, K, 512 * half)
                fm_pass(units, rhs, n, banks)
                for oc in range(4):
                    kc = 4 * half + oc
                    STT(xT[:, kc, :n], xT[:, kc, :n], DN_ALPHA, ps[banks[oc]][:, :n], ALU.mult, ALU.add,
                        [psb[banks[oc]], xTB[kc]], [xTB[kc]])
            for kc in range(8):
                tb, tbb = RB.next()
                CP("act", tb[:, :n], xT[:, kc, :n], [xTB[kc]], [tbb])
                sq, sqb = RB.next()
                TT(sq[:, :n], xT[:, kc, :n], xT[:, kc, :n], ALU.mult, [xTB[kc]], [sqb])
                MM(ps[0][:, :n], meanm[:], tb[:, :n], kc == 0, kc == 7, [CB, tbb], [psb[0]])
                MM(ps[1][:, :n], meanm[:], sq[:, :n], kc == 0, kc == 7, [CB, sqb], [psb[1]])
            CP("act", MEAN[:, :n], ps[0][:, :n], [psb[0]], [MEANb])
            TT(MR[:, :n], MEAN[:, :n], MEAN[:, :n], ALU.mult, [MEANb], [MRb])
            TT(RSTD[:, :n], ps[1][:, :n], MR[:, :n], ALU.subtract, [psb[1], MRb], [RSTDb])
            ACTF(RSTD[:, :n], RSTD[:, :n], AF.Sqrt, [RSTDb], [RSTDb], bias=LN_EPS)
            S.op("dve", lambda e: e.reciprocal(out=RSTD[:, :n], in_=RSTD[:, :n]), [RSTDb], [RSTDb])
            TT(MR[:, :n], MEAN[:, :n], RSTD[:, :n], ALU.mult, [MEANb, RSTDb], [MRb])
            gi = O_LNG + (layer * 2 + which) * 8
            bi = O_LNB + (layer * 2 + which) * 8
            for kc in range(8):
                t, tb_ = RF.next()
                TT(t[:, :n], xT[:, kc, :n], RSTD[:, :n], ALU.mult, [xTB[kc], RSTDb], [tb_])
                TT(t[:, :n], t[:, :n], MR[:, :n], ALU.subtract, [tb_, MRb], [tb_])
                TS(xT[:, kc, :n], t[:, :n], vcol(gi + kc), vcol(bi + kc), ALU.mult, ALU.add, [tb_, VB], [xTB[kc]])
                CP("act", xTb[:, kc, :n], xT[:, kc, :n], [xTB[kc]], [xTbB[kc]])

        def seg_layout(segs, H):
            out = []
            for si, (c0, m, kind) in enumerate(segs):
                base = 0 if si == 0 else (H + 16) + (si - 1) * (H + 8)
                out.append((base, c0, m, kind))
            tot = out[-1][0] + H + out[-1][2]
            return out, tot

        def halo_evac(bank, bankb, segs, H, halo_ap, halo_b, samp_ap, update=True):
            lay, tot = seg_layout(segs, H)
            w, wb = RF.next()
            for si, (base, c0, m, kind) in enumerate(lay):
                if kind == "p":
                    CP("dve", w[:, base:base + H], halo_ap, [halo_b], [wb])
                else:
                    CP("dve", w[:, base:base + H], samp_ap(int(kind[1])), [SSb], [wb])
                CP("act", w[:, base + H:base + H + m], bank[:, c0:c0 + m], [bankb], [wb])
            if update and len(segs) == 1:
                m = segs[0][1]
                CP("dve", halo_ap, w[:, m:m + H], [wb], [halo_b])
            return w, wb, lay, tot

        def ffn(layer, n, segs, tail):
            W = fup[layer]
            for cg in range(11):
                banks = [0, 1, 2, 3] if cg % 2 == 0 else [4, 5, 6, 7]
                units = load_units(W, 8, 512 * cg)
                fm_pass(units, xrhs, n, banks)
                for i in range(4):
                    ch = 4 * cg + i
                    bk, bkb = ps[banks[i]], psb[banks[i]]
                    if ch < NFC:
                        hi = layer * NFC + ch
                        w, wb, lay, tot = halo_evac(bk, bkb, segs, 2, GHALO[:, hi, :], GHb[hi],
                                                    lambda s, hi=hi: SG[:, hi, 2 * s:2 * s + 2])
                        wo = O_FCW + layer * 66 + ch
                        t, tb_ = RF.next()
                        TS(t[:, 2:tot], w[:, 2:tot], vcol(wo + 44), vcol(O_FCB + layer * NFC + ch), ALU.mult, ALU.add,
                           [wb, VB], [tb_])
                        STT(t[:, 2:tot], w[:, 1:tot - 1], vcol(wo + 22), t[:, 2:tot], ALU.mult, ALU.add, [wb, tb_, VB], [tb_])
                        g2, g2b = RF.next()
                        for (base, c0, m, kind) in lay:
                            STT(g2[:, c0:c0 + m], w[:, base:base + m], vcol(wo), t[:, base + 2:base + 2 + m],
                                ALU.mult, ALU.add, [wb, tb_, VB], [g2b])
                        ACTF(gg[:, ch, :n], g2[:, :n], AF.Gelu, [g2b], [ggB[ch]])
                    else:
                        c2 = ch - NFC
                        TT(gg[:, c2, :n], gg[:, c2, :n], bk[:, :n], ALU.mult, [ggB[c2], bkb], [ggB[c2]])
                if tail and cg < 6:
                    tsegs = [(c0, m) for (c0, m, k) in segs]
                    tbk = [4, 5, 6, 7, 3] if cg % 2 == 0 else [0, 1, 2, 3, 7]
                    width = 512 if cg < 5 else 256
                    tm_pass(units, tsegs, tbk, width)
                    for si, (c0, m) in enumerate(tsegs):
                        st, stb = RF.next()
                        CP("act", st[:m, :width], ps[tbk[si]][:m, :width], [psb[tbk[si]]], [stb])
                        if si == 0:
                            DMA("sp", fcp[layer, :, 512 * cg:512 * cg + width], st[m - 2:m, :width], [stb], [])
                        else:
                            DMA("sp", fcs[layer, 2 * (si - 1):2 * si, 512 * cg:512 * cg + width], st[m - 2:m, :width], [stb], [])
            residual_ln(fdn[layer], NFC, lambda kc: (gg[:, kc, :], ggB[kc]), n, layer, 1)

        def attn_main(g, n):
            jmax = 4 * g + 3
            zr = [0]
            for h in range(8):
                p, r0 = h // 2, 64 * (h % 2)
                ob = 4 + (h % 2)
                a_, ab_ = h % 2, h % 2
                MM(ps[ob][:, :n], zerosB[:], qT[:, p, :n], True, False, [CB, qTB[p]], [psb[ob]])
                MEMSET("dve", ACC[:, a_, :], 0.0, [ACCb[a_]])
                hbias = vcol(O_HB + h)
                st = {}

                def stage1a(j):
                    z = zr[0]
                    zr[0] = (z + 1) % 4
                    c0 = max(0, 128 * (j - 4 * g))
                    diag = j >= 4 * g
                    MM(ps[z][:, c0:n], KT[r0:r0 + 64, p, 128 * j:128 * j + 128], qT[r0:r0 + 64, p, c0:n], True, False,
                       [KTB[p], qTB[p]], [psb[z]], signal=not diag)
                    if diag:
                        MM(ps[z][:, c0:c0 + 128], identB[:], maskM[:], False, False, [CB], [psb[z]])
                    st[j] = [z, c0, diag, None, None]

                def stage1b(j):
                    z, c0, diag, _, _ = st[j]
                    E, Eb = RF.next()
                    ACTF(E[:, c0:n], ps[z][:, c0:n], AF.Exp, [psb[z], VB], [Eb], bias=hbias)
                    Lp, Lpb = RB.next()
                    ACTF(Lp[:, c0:n], E[:, c0:n], AF.Ln, [Eb], [Lpb], bias=1.0)
                    st[j][3], st[j][4] = Lp, Lpb

                def stage2(j):
                    z, c0, diag, Lp, Lpb = st.pop(j)
                    cc0 = c0 + 128 if diag else 0
                    carry = (j != jmax) and cc0 < n
                    MM(ps[z][:, c0:n], negtri[:], Lp[:, c0:n], False, not carry, [CB, Lpb], [psb[z]], signal=not carry)
                    if carry:
                        MM(ps[z][:, cc0:n], negones[:], ACCB[:, ab_, cc0:n], False, True, [CB, ACCBb[ab_]], [psb[z]])
                    A, Ab = RB.next()
                    ACTF(A[:, c0:n], ps[z][:, c0:n], AF.Exp, [psb[z], VB], [Ab], bias=hbias)
                    MM(ps[ob][:, c0:n], V[:, j, 128 * p:128 * p + 128], A[:, c0:n], False, j == 0, [VBl[j], Ab], [psb[ob]])
                    if j > 0:
                        TT(ACC[:, a_, c0:n], ACC[:, a_, c0:n], Lp[:, c0:n], ALU.add, [ACCb[a_], Lpb], [ACCb[a_]])
                        CP("dve", ACCB[:, ab_, c0:n], ACC[:, a_, c0:n], [ACCb[a_]], [ACCBb[ab_]])

                stage1a(jmax)
                if jmax >= 1:
                    stage1a(jmax - 1)
                stage1b(jmax)
                for j in range(jmax, -1, -1):
                    if j >= 2:
                        stage1a(j - 2)
                    if j >= 1:
                        stage1b(j - 1)
                    stage2(j)
                CP("dve", cat(p)[r0:r0 + 64, :n], ps[ob][r0:r0 + 64, :n], [psb[ob]], [catB[p]])

        def attn_batched(items, nq, nblocks, get_block, maskQ, hb0):
            ni = len(items)
            Wd = ni * 8 * nq
            ob = 3
            MM(ps[ob][:, :Wd], zerosB[:], maskQ[:, :Wd], True, False, [CB], [psb[ob]])
            MEMSET("dve", ACC[:, 0, :], 0.0, [ACCb[0]])
            zr = [0]
            st = {}

            def stage1(j):
                blk, nk, diag = get_block(j)
                z = zr[0]
                zr[0] = (z + 1) % 3
                first = True
                for it, (qc0, oc0) in enumerate(items):
                    kt, ktb, v_ap, vb = blk[it]
                    for p in range(4):
                        col = (it * 8 + 2 * p) * nq
                        MM(ps[z][:nk, col:col + 2 * nq].rearrange("k (h q) -> k h q", h=2), kt(p)[:, :nk],
                           QZ[:, 2 * p:2 * p + 2, qc0:qc0 + nq], first, False,
                           list(ktb) + [QZB], [psb[z]], signal=False)
                        first = False
                MM(ps[z][:nk, :Wd], onesB[0:1, :nk], HBH[0:1, hb0:hb0 + Wd], False, False, [CB, VB], [psb[z]], signal=False)
                MM(ps[z][:nk, :Wd], onesB[0:1, :nk], HBL[0:1, hb0:hb0 + Wd], False, False, [CB, VB], [psb[z]], signal=not diag)
                if diag:
                    MM(ps[z][:nk, :Wd], identB[:nk, :nk], maskQ[:nk, :Wd], False, False, [CB], [psb[z]])
                E, Eb = RF.next()
                ACTF(E[:nk, :Wd], ps[z][:nk, :Wd], AF.Exp, [psb[z]], [Eb])
                Lp, Lpb = RB.next()
                ACTF(Lp[:nk, :Wd], E[:nk, :Wd], AF.Ln, [Eb], [Lpb], bias=1.0)
                st[j] = (z, blk, nk, Lp, Lpb)

            def stage2(j):
                z, blk, nk, Lp, Lpb = st.pop(j)
                carry = j != nblocks - 1
                MM(ps[z][:nk, :Wd], negtri[:nk, :nk], Lp[:nk, :Wd], False, not carry, [CB, Lpb], [psb[z]], signal=not carry)
                if carry:
                    MM(ps[z][:nk, :Wd], negones[:, :nk], ACCB[:, 0, :Wd], False, True, [CB, ACCBb[0]], [psb[z]])
                A, Ab = RB.next()
                ACTF(A[:nk, :Wd], ps[z][:nk, :Wd], AF.Exp, [psb[z]], [Ab])
                cnt = 0
                for it in range(ni):
                    kt, ktb, v_ap, vb = blk[it]
                    for p in range(4):
                        col = (it * 8 + 2 * p) * nq
                        cnt += 1
                        MM(ps[ob][:, col:col + 2 * nq], v_ap[:nk, 128 * p:128 * p + 128], A[:nk, col:col + 2 * nq], False,
                           (j == 0 and cnt == ni * 4), list(vb) + [Ab], [psb[ob]], signal=(cnt == ni * 4))
                if j > 0:
                    TT(ACC[:nk, 0, :Wd], ACC[:nk, 0, :Wd], Lp[:nk, :Wd], ALU.add, [ACCb[0], Lpb], [ACCb[0]])
                    CP("dve", ACCB[:, 0, :Wd], ACC[:, 0, :Wd], [ACCb[0]], [ACCBb[0]])

            stage1(nblocks - 1)
            for j in range(nblocks - 1, -1, -1):
                if j > 0:
                    stage1(j - 1)
                stage2(j)
            for it, (qc0, oc0) in enumerate(items):
                for h in range(8):
                    p, r0 = h // 2, 64 * (h % 2)
                    col = (it * 8 + h) * nq
                    CP("dve", cat(p)[r0:r0 + 64, oc0:oc0 + nq], ps[ob][r0:r0 + 64, col:col + nq], [psb[ob]], [catB[p]])

        def load_sample_states():
            def tr_rows(src, rows, nchunks, dst_fn):
                w = nchunks * 128
                view = xin[:rows, :, :].rearrange("p a b -> p (a b)")
                DMA("sp", view[:, :w], src, [], xinb)
                for c in range(nchunks):
                    bk = c % 8
                    TR(ps[bk][:, :rows], view[:, 128 * c:128 * c + 128], identF[:rows, :rows], xinb + [CB], [psb[bk]])
                    CP("dve", dst_fn(c), ps[bk][:, :rows], [psb[bk]], [SSb])
            tr_rows(spool, 60, 4, lambda c: SU[:, c, :])
            tr_rows(slc, 12, 8, lambda c: SR[:, c, :])
            tr_rows(slh, 4, 8, lambda c: SH[:, c, :])
            for l in range(2):
                tr_rows(sfc[l], 8, NFC, lambda c, l=l: SG[:, l * NFC + c, :])
            for s in range(4):
                t, tb_ = RF.next()
                DMA("sp", t[:7, :512], spool[15 * s + 8:15 * s + 15, :], [], [tb_])
                DMA("sp", pools[15 * s:15 * s + 7, :], t[:7, :512], [tb_], [])

        groups = [(512, False, 512 * g) for g in range(NG)] + [(48, True, 4096)]
        S.mark('consts')
        load_sample_states()
        S.mark('states')
        for gi_, (n, tail, pos0) in enumerate(groups):
            if not tail:
                segs = [(0, 512, "p")]
                tsegs = [(128 * i, 128) for i in range(4)]
            else:
                segs = [(0, 16, "p")] + [(16 + 8 * s, 8, "s%d" % s) for s in range(4)]
                tsegs = [(c0, m) for (c0, m, k) in segs]
            npr = 512 if not tail else 16

            xv = xin[:, :, :].rearrange("p a b -> p (a b)")
            if not tail:
                for i in range(4):
                    rows0 = pos0 + 128 * i
                    dstv = xin[:, 2 * i:2 * i + 2, :].rearrange("p a b -> p (a b)")
                    if rows0 == 0:
                        DMA("sp", dstv[0:16, :], meta, [], xinb[0:2])
                        DMA("sp", dstv[16:128, :], xp[0:112, :], [], xinb[0:2])
                    else:
                        DMA("sp", dstv[:, :], xp[rows0 - 16:rows0 + 112, :], [], xinb[2 * i:2 * i + 2])
                for kc in range(8):
                    for i in range(4):
                        dstv = xin[:, 2 * i:2 * i + 2, :].rearrange("p a b -> p (a b)")
                        TR(ps[kc][:, 128 * i:128 * i + 128], dstv[:, 128 * kc:128 * kc + 128], identF[:], xinb[2 * i:2 * i + 2] + [CB], [psb[kc]])
            else:
                dstv = xin[:, 0:2, :].rearrange("p a b -> p (a b)")
                DMA("sp", dstv[0:16, :], xp[SEQ - 16:SEQ, :], [], xinb[0:2])
                DMA("sp", dstv[16:48, :], xs, [], xinb[0:2])
                for kc in range(8):
                    TR(ps[kc][:, 0:48], dstv[0:48, 128 * kc:128 * kc + 128], identF[:48, :48], xinb[0:2] + [CB], [psb[kc]])
            for kc in range(8):
                CP("act", xT[:, kc, :n], ps[kc][:, :n], [psb[kc]], [xTB[kc]])
                CP("dve", xTb[:, kc, :n], ps[kc][:, :n], [psb[kc]], [xTbB[kc]])

            S.mark('g%d:G0' % gi_)
            units = load_units(w_in0, 8, 0)
            fm_pass(units, xrhs, n, [0, 1, 2, 3])
            for oc in range(4):
                ACTF(qT[:, oc, :n], ps[oc][:, :n], AF.Copy, [psb[oc]], [qTB[oc]], scale=0.125)
            if tail:
                MEMSET("dve", QZ[:], 0.0, [QZB])
                for h in range(8):
                    r0 = 64 * (h % 2)
                    CP("dve", QZ[r0:r0 + 64, h, :], qT[r0:r0 + 64, h // 2, 0:48], [qTB[h // 2]], [QZB])
            units = load_units(w_in0, 8, 512)
            fm_pass(units, xrhs, n, [4, 5, 6, 7])
            for oc in range(4):
                CP("dve", KT[:, oc, pos0:pos0 + npr], ps[4 + oc][:, :npr], [psb[4 + oc]], [KTB[oc]])
                if tail:
                    CP("dve", KTS[:, oc, :], ps[4 + oc][:, 16:48], [psb[4 + oc]], [KTSB])
            kb = [0, 1, 2, 3] if not tail else [0, 1, 2, 3, 4]
            tm_pass(units, tsegs, kb)
            for si, (c0, m) in enumerate(tsegs):
                st, stb = RF.next()
                CP("act", st[:m, :512], ps[kb[si]][:m, :], [psb[kb[si]]], [stb])
                if not tail or si == 0:
                    DMA("sp", kp[pos0 + c0:pos0 + c0 + m, :], st[:m, :512], [stb], [])
                else:
                    DMA("sp", ks[8 * (si - 1):8 * si, :], st[:m, :512], [stb], [])
            units = load_units(w_in0, 8, 1024)
            vb_ = [4, 5, 6, 7] if not tail else [5, 6, 7, 3, 4]
            tm_pass(units, tsegs, vb_)
            for si, (c0, m) in enumerate(tsegs):
                st, stb = RF.next()
                CP("act", st[:m, :512], ps[vb_[si]][:m, :], [psb[vb_[si]]], [stb])
                if not tail or si == 0:
                    blk = (pos0 + c0) // 128
                    CP("dve", V[:m, blk, :], ps[vb_[si]][:m, :], [psb[vb_[si]]], [VBl[blk]])
                    DMA("sp", vp[pos0 + c0:pos0 + c0 + m, :], st[:m, :512], [stb], [])
                else:
                    CP("dve", VSN[:, si - 1, :], ps[vb_[si]][:m, :], [psb[vb_[si]]], [VSNB[si - 1]])
                    DMA("sp", vs[8 * (si - 1):8 * si, :], st[:m, :512], [stb], [])
            units = load_units(w_in0, 8, 1536)
            fm_pass(units, xrhs, n, [0, 1, 2, 3])
            for c in range(4):
                w, wb, lay, tot = halo_evac(ps[c], psb[c], segs, 15, UHALO[:, c, :], UHb[c],
                                            lambda s, c=c: SU[:, c, 15 * s:15 * s + 15])
                wn = 2 << c
                a, ab = RF.next()
                TT(a[:, 1:tot], w[:, 1:tot], w[:, 0:tot - 1], ALU.add, [wb], [ab])
                cur, curb, sh = a, ab, 2
                while sh < wn:
                    nb, nbb = RF.next()
                    lo = 2 * sh - 1
                    TT(nb[:, lo:tot], cur[:, lo:tot], cur[:, lo - sh:tot - sh], ALU.add, [curb], [nbb])
                    cur, curb, sh = nb, nbb, 2 * sh
                dd, ddb = RB.next()
                for (base, c0, m, kind) in lay:
                    o = base + 15
                    STT(dd[:, c0:c0 + m], cur[:, o:o + m], 1.0 / wn, w[:, o:o + m], ALU.mult, ALU.subtract, [curb, wb], [ddb])
                    if gi_ == 0:
                        t2, t2b = RF.next()
                        TT(t2[:, 0:16], cur[:, o:o + 16], RC[:, c, :], ALU.mult, [curb, VB], [t2b])
                        TT(dd[:, c0:c0 + 16], t2[:, 0:16], w[:, o:o + 16], ALU.subtract, [t2b, wb], [ddb])
                bk = 6 + (c % 2)
                MM(ps[bk][:, :n], PW[:, c, :], dd[:, :n], True, True, [PWB, ddb], [psb[bk]])
                TS(cat(4 + c)[:, :n], ps[bk][:, :n], vcol(O_PSC + c), None, ALU.mult, None, [psb[bk], VB], [catB[4 + c]])
            if tail:
                ub = [4, 5, 6, 7, 0]
                tm_pass(units, tsegs, ub)
                for si, (c0, m) in enumerate(tsegs):
                    st, stb = RF.next()
                    CP("act", st[:m, :512], ps[ub[si]][:m, :], [psb[ub[si]]], [stb])
                    if si == 0:
                        DMA("sp", poolp[:, :], st[1:16, :512], [stb], [])
                    else:
                        DMA("sp", pools[15 * (si - 1) + 7:15 * si, :], st[:8, :512], [stb], [])
            S.mark('g%d:inproj+pool' % gi_)
            if not tail:
                attn_main(gi_, n)
            else:
                def blk_prompt(j):
                    nk = 128 if j < 32 else 16
                    return ([(lambda p, j=j, nk=nk: KT[:, p, 128 * j:128 * j + nk], KTB, V[:, j, :], [VBl[j]])], nk, j == 32)
                attn_batched([(0, 0)], 16, NBLK, blk_prompt, maskQ16, 256)

                pg_state = {}

                def blk_sample(j):
                    if j == PAGES:
                        return ([(lambda p, s=s: KTS[:, p, 8 * s:8 * s + 8], [KTSB], VSN[:, s, :], [VSNB[s]]) for s in range(4)], 8, True)
                    res = []
                    for s in range(4):
                        col = s * 128 + j
                        kpg, kpb = xin[:, 2 * s, :], xinb[2 * s]
                        vpg, vpb = xin[:, 2 * s + 1, :], xinb[2 * s + 1]
                        S.dma("pool", (lambda kpg=kpg, col=col: (lambda e: e.indirect_dma_start(
                            out=kpg, out_offset=None, in_=ck, in_offset=bass.IndirectOffsetOnAxis(ap=IDX[:, col:col + 1], axis=0))))(),
                            [IDXb], [kpb])
                        S.dma("pool", (lambda vpg=vpg, col=col: (lambda e: e.indirect_dma_start(
                            out=vpg, out_offset=None, in_=cv, in_offset=bass.IndirectOffsetOnAxis(ap=IDX[:, col:col + 1], axis=0))))(),
                            [IDXb], [vpb])
                        tb = 4 + s
                        for p in range(4):
                            TR(ps[tb][:, 128 * p:128 * p + 128], kpg[:, 128 * p:128 * p + 128], identF[:], [kpb, CB], [psb[tb]])
                        vch = (4 + s) if (j % 2 == 0) else (16 + s)
                        CP("dve", gg[:, s, :], ps[tb][:, :], [psb[tb]], [ggB[s]])
                        CP("dve", gg[:, vch, :], vpg, [vpb], [ggB[vch]])
                        res.append((lambda p, s=s: gg[:, s, 128 * p:128 * p + 128], [ggB[s]], gg[:, vch, :], [ggB[vch]]))
                    return (res, 128, False)
                attn_batched([(16 + 8 * s, 16 + 8 * s) for s in range(4)], 8, PAGES + 1, blk_sample, maskQ8, 0)
            S.mark('g%d:attn' % gi_)
            residual_ln(w_out0, 8, lambda kc: (cat(kc), catB[kc]), n, 0, 0)
            S.mark('g%d:ln1' % gi_)
            ffn(0, n, segs, tail)
            S.mark('g%d:ffn0' % gi_)

            for cg in range(2):
                banks = [0, 1, 2, 3] if cg == 0 else [4, 5, 6, 7]
                units = load_units(lw_in, 8, 512 * cg)
                fm_pass(units, xrhs, n, banks)
                for i in range(4):
                    c = 4 * cg + i
                    ACTF(gg[:, c, :n], ps[banks[i]][:, :n], AF.Gelu, [psb[banks[i]]], [ggB[c]])
            for cg in range(2, 4):
                banks = [0, 1, 2, 3] if cg == 2 else [4, 5, 6, 7]
                units = load_units(lw_in, 8, 512 * cg)
                fm_pass(units, xrhs, n, banks)
                if tail:
                    tbk = [4, 5, 6, 7, 3] if cg == 2 else [0, 1, 2, 3, 7]
                    tm_hold = (units, tbk, cg)
                else:
                    tm_hold = None
                for i in range(4):
                    c = 4 * (cg - 2) + i
                    bk, bkb = ps[banks[i]], psb[banks[i]]
                    w, wb, lay, tot = halo_evac(bk, bkb, segs, 3, RHALO[:, c, :], RHb[c], lambda s, c=c: SR[:, c, 3 * s:3 * s + 3])
                    t, tb_ = RF.next()
                    TS(t[:, 3:tot], w[:, 3:tot], vcol(O_LCW + 24 + c), vcol(O_LCB + c), ALU.mult, ALU.add, [wb, VB], [tb_])
                    STT(t[:, 3:tot], w[:, 2:tot - 1], vcol(O_LCW + 16 + c), t[:, 3:tot], ALU.mult, ALU.add, [wb, tb_, VB], [tb_])
                    STT(t[:, 3:tot], w[:, 1:tot - 2], vcol(O_LCW + 8 + c), t[:, 3:tot], ALU.mult, ALU.add, [wb, tb_, VB], [tb_])
                    xc, xcb = RF.next()
                    for (base, c0, m, kind) in lay:
                        STT(xc[:, c0:c0 + m], w[:, base:base + m], vcol(O_LCW + c), t[:, base + 3:base + 3 + m],
                            ALU.mult, ALU.add, [wb, tb_, VB], [xcb])
                    xb16, xb16b = RB.next()
                    CP("act", xb16[:, :n], xc[:, :n], [xcb], [xb16b])
                    bnk = banks[i]
                    r, rb = RF.next()
                    MM(ps[bnk][:, :n], LWA[:, c, :], xb16[:, :n], True, True, [LWB, xb16b], [psb[bnk]])
                    ACTF(r[:, :n], ps[bnk][:, :n], AF.Sigmoid, [psb[bnk], VB], [rb], bias=vcol(O_LBA + c))
                    gi2, gib = RF.next()
                    MM(ps[bnk][:, :n], LWX[:, c, :], xb16[:, :n], True, True, [LWB, xb16b], [psb[bnk]])
                    ACTF(gi2[:, :n], ps[bnk][:, :n], AF.Sigmoid, [psb[bnk], VB], [gib], bias=vcol(O_LBX + c))
                    a, ab = RF.next()
                    ACTF(a[:, :n], r[:, :n], AF.Tanh, [rb, VB], [ab], scale=SP4[:, c:c + 1])
                    t1, t1b = RF.next()
                    TS(t1[:, :n], a[:, :n], -1.0, 1.0, ALU.mult, ALU.add, [ab], [t1b])
                    S.op("dve", (lambda t1=t1, n=n: (lambda e: e.reciprocal(out=t1[:, :n], in_=t1[:, :n])))(), [t1b], [t1b])
                    TS(a[:, :n], a[:, :n], 1.0, None, ALU.add, None, [ab], [ab])
                    TT(a[:, :n], a[:, :n], t1[:, :n], ALU.mult, [ab, t1b], [ab])
                    ACTF(r[:, :n], r[:, :n], AF.Tanh, [rb, VB], [rb], scale=SP8[:, c:c + 1])
                    TT(t1[:, :n], a[:, :n], a[:, :n], ALU.mult, [ab, t1b], [t1b])
                    STT(t1[:, :n], t1[:, :n], 1.0, r[:, :n], ALU.add, ALU.mult, [t1b, rb], [t1b])
                    ACTF(t1[:, :n], t1[:, :n], AF.Sqrt, [t1b], [t1b], scale=-1.0)
                    TT(gi2[:, :n], gi2[:, :n], xc[:, :n], ALU.mult, [gib, xcb], [gib])
                    TT(gi2[:, :n], gi2[:, :n], t1[:, :n], ALU.mult, [gib, t1b], [gib])
                    hh, hb_ = RF.next()
                    for si, (base, c0, m, kind) in enumerate(lay):
                        if kind == "p":
                            init, initb = HST[:, c:c + 1], HSTb[c]
                        else:
                            s_ = int(kind[1])
                            init, initb = SH[:, c, s_:s_ + 1], SSb
                        S.op("dve", (lambda c0=c0, m=m, init=init, hh=hh, a=a, gi2=gi2: (lambda e: e.tensor_tensor_scan(
                            out=hh[:, c0:c0 + m], data0=a[:, c0:c0 + m], data1=gi2[:, c0:c0 + m], initial=init,
                            op0=ALU.mult, op1=ALU.add)))(), [ab, gib, initb], [hb_])
                        if kind == "p":
                            CP("dve", HST[:, c:c + 1], hh[:, c0 + m - 1:c0 + m], [hb_], [HSTb[c]])
                        if tail:
                            CP("dve", HL[:, c, si:si + 1], hh[:, c0 + m - 1:c0 + m], [hb_], [HLb])
                    TT(cat(c)[:, :n], gg[:, c, :n], hh[:, :n], ALU.mult, [ggB[c], hb_], [catB[c]])
                if tm_hold is not None:
                    units_, tbk, cg_ = tm_hold
                    tm_pass(units_, tsegs, tbk)
                    for si, (c0, m) in enumerate(tsegs):
                        st, stb = RF.next()
                        CP("act", st[:m, :512], ps[tbk[si]][:m, :], [psb[tbk[si]]], [stb])
                        cc = 512 * (cg_ - 2)
                        if si == 0:
                            DMA("sp", lcp[:, cc:cc + 512], st[m - 3:m, :512], [stb], [])
                        else:
                            DMA("sp", lcs[3 * (si - 1):3 * si, cc:cc + 512], st[m - 3:m, :512], [stb], [])
            if tail:
                for c in range(8):
                    bk = 0 if c < 4 else 1
                    TR(ps[bk][:5, 128 * (c % 4):128 * (c % 4) + 128], HL[:, c, :], identF[:], [HLb, CB], [psb[bk]])
                for half in range(2):
                    st, stb = RF.next()
                    CP("act", st[:5, :512], ps[half][:5, :], [psb[half]], [stb])
                    DMA("sp", lhp[:, 512 * half:512 * half + 512], st[0:1, :512], [stb], [])
                    DMA("sp", lhs[:, 512 * half:512 * half + 512], st[1:5, :512], [stb], [])
            S.mark('g%d:lru' % gi_)
            residual_ln(lw_out, 8, lambda kc: (cat(kc), catB[kc]), n, 1, 0)
            S.mark('g%d:ln3' % gi_)
            ffn(1, n, segs, tail)
            S.mark('g%d:ffn1' % gi_)

            for si, (c0, m) in enumerate(tsegs if not tail else [(0, 48)]):
                b0 = 2 * (si % 4)
                for kc in range(8):
                    bk = b0 + (kc // 4)
                    TR(ps[bk][:m, 128 * (kc % 4):128 * (kc % 4) + 128], xT[:, kc, c0:c0 + m], identF[:], [xTB[kc], CB], [psb[bk]])
                for half in range(2):
                    st, stb = RF.next()
                    CP("act" if half == 0 else "dve", st[:m, :512], ps[b0 + half][:m, :], [psb[b0 + half]], [stb])
                    cs = slice(512 * half, 512 * half + 512)
                    if not tail:
                        r0_ = pos0 + c0 - 16
                        if r0_ < 0:
                            DMA("sp", yp[0:112, cs], st[16:128, :512], [stb], [])
                        else:
                            DMA("sp", yp[r0_:r0_ + 128, cs], st[:128, :512], [stb], [])
                    else:
                        DMA("sp", yp[SEQ - 16:SEQ, cs], st[0:16, :512], [stb], [])
                        DMA("sp", ys[:, cs], st[16:48, :512], [stb], [])

        S.finish()
        build_program.stats = {e: len(S.prog[e]) for e in ENGS}
        build_program.marks = S.marks
        with nc.Block() as block:
            S.emit(block)
    return nc


_NC_CACHE = {}


def kernel(**inp):
    f = lambda a: np.ascontiguousarray(np.asarray(a, dtype=np.float32))
    x_prompt = f(inp["x_prompt"]); x_sample = f(inp["x_sample"])
    ck = f(inp["cache_sb_k"]).reshape(NPHYS * 128, 512)
    cv = f(inp["cache_sb_v"]).reshape(NPHYS * 128, 512)
    page_table = np.ascontiguousarray(np.asarray(inp["page_table"], dtype=np.int32))

    def cm(v):
        return np.ascontiguousarray(f(v).reshape(-1, 128).T)
    hb = f(inp["sb_logit_bias"])[0]
    vec = np.concatenate([
        cm(inp["ln_g"]), cm(inp["ln_b"]), cm(inp["pool_scale"]), cm(inp["lru_conv_w"]), cm(inp["lru_conv_b"]),
        cm(inp["lru_b_a"]), cm(inp["lru_b_x"]), cm(inp["lru_lambda"]), cm(inp["ffn_conv_w"]), cm(inp["ffn_conv_b"]),
        np.tile(hb[None, :], (128, 1))], axis=1).astype(np.float32)
    assert vec.shape == (128, VR)
    hbr = np.concatenate([np.tile(np.repeat(hb, 8), 4), np.repeat(hb, 16)])[None, :].astype(np.float32)

    common = dict(
        meta=f(inp["meta_tokens"]), ck=ck, cv=cv,
        w_in0=f(inp["sb_w_in"])[0], w_out0=f(inp["sb_w_out"])[0], poolw=f(inp["pool_w"])[0].reshape(512, 128),
        lw_in=f(inp["lru_w_in"])[0], lwa=f(inp["lru_w_a"])[0].reshape(1024, 128), lwx=f(inp["lru_w_x"])[0].reshape(1024, 128),
        lw_out=f(inp["lru_w_out"])[0], fup0=f(inp["ffn_w_up"])[0], fup1=f(inp["ffn_w_up"])[1],
        fdn0=f(inp["ffn_w_down"])[0], fdn1=f(inp["ffn_w_down"])[1], vec=vec, hbr=hbr)
    in_maps = []
    for c in range(8):
        b = c % 4
        sl = slice(4 * c, 4 * c + 4)
        m = dict(common)
        m.update(
            xp=x_prompt[b], xs=np.ascontiguousarray(x_sample[sl].reshape(32, D)),
            pt=np.ascontiguousarray(page_table[sl]),
            spool=np.ascontiguousarray(f(inp["state_pool"])[0, sl].reshape(60, 512)),
            slc=np.ascontiguousarray(f(inp["state_lru_conv"])[0, sl].reshape(12, D)),
            slh=np.ascontiguousarray(f(inp["state_lru_h"])[0, sl]),
            sfc=np.ascontiguousarray(f(inp["state_ffn_conv"])[:, sl].reshape(2, 8, FFN)))
        in_maps.append(m)
    if "nc" not in _NC_CACHE:
        _NC_CACHE["nc"] = build_program()
    res = run_bass_kernel_spmd(_NC_CACHE["nc"], in_maps, core_ids=list(range(8)))
    R = res.results
    pr = lambda k: np.stack([R[b][k] for b in range(4)])
    sm = lambda k: np.concatenate([R[c][k] for c in range(8)], axis=0)
    y_prompt = pr("yp")
    y_sample = sm("ys").reshape(32, 8, D)
    k_prompt = pr("kp").reshape(1, 4, L, 8, 64); v_prompt = pr("vp").reshape(1, 4, L, 8, 64)
    k_sample = sm("ks").reshape(1, 32, 8, 8, 64); v_sample = sm("vs").reshape(1, 32, 8, 8, 64)
    pool_prompt = pr("poolp").reshape(1, 4, 15, 512); pool_sample = sm("pools").reshape(1, 32, 15, 512)
    lcp = pr("lcp").reshape(1, 4, 3, D); lcs = sm("lcs").reshape(1, 32, 3, D)
    lhp = pr("lhp").reshape(1, 4, D); lhs = sm("lhs").reshape(1, 32, D)
    fcp = np.stack([R[b]["fcp"] for b in range(4)], axis=1)
    fcs = np.concatenate([R[c]["fcs"].reshape(2, 4, 2, FFN) for c in range(8)], axis=1)
    outs = (y_prompt, y_sample, k_prompt, v_prompt, k_sample, v_sample, pool_prompt, pool_sample,
            lcp, lcs, lhp, lhs, fcp, fcs)
    return tuple(np.ascontiguousarray(o, dtype=np.float32) for o in outs)
```

```python
import numpy as np
from contextlib import ExitStack
import concourse.bass as bass
import concourse.mybir as mybir
from concourse.bass_utils import run_bass_kernel_spmd

F32 = mybir.dt.float32
BF16 = mybir.dt.bfloat16
I32 = mybir.dt.int32
U32 = mybir.dt.uint32
AF = mybir.ActivationFunctionType
ALU = mybir.AluOpType

D = 1024
SEQ = 4096
NMETA = 16
L = SEQ + NMETA
NT = 512
NG = 8
NBLK = 33
FFN = 2816
NFC = 22
PAGES = 128
NPHYS = 5120
DN_ALPHA = (2.0 * 2) ** 0.25
LN_EPS = 1e-5
NEG = -30000.0
LIMIT = 0

O_LNG, O_LNB, O_PSC, O_LCW, O_LCB, O_LBA, O_LBX, O_LAM, O_FCW, O_FCB, O_HB, VR = (
    0, 32, 64, 68, 100, 108, 116, 124, 132, 264, 308, 316)

ENGS = ("pe", "act", "dve", "pool", "sp")


class Buf:
    __slots__ = ("w", "r", "excl")

    def __init__(self, excl=False):
        self.w = None
        self.r = {}
        self.excl = excl


class Sched:
    K = 8

    def __init__(self, nc, es):
        self.nc = nc
        self.prog = {e: [] for e in ENGS}
        self.sem = {e: es.enter_context(nc.semaphore("s_" + e)) for e in ENGS}
        self.cnt = {e: 0 for e in ENGS}
        self.seen = {e: {} for e in ENGS}
        self.ring = {q: [es.enter_context(nc.semaphore("r_%s_%d" % (q, i))) for i in range(self.K)]
                     for q in ("sp", "pool")}
        self.rcnt = {q: 0 for q in ("sp", "pool")}
        self.seq = 0
        self.limit = LIMIT
        self.marks = []

    def _waits(self, e, toks):
        need = {}
        for t in toks:
            if t is None:
                continue
            sem, val, te = t
            if te == "pe" and e == "pe":
                continue
            k = id(sem)
            if self.seen[e].get(k, 0) >= val:
                continue
            if k not in need or need[k][1] < val:
                need[k] = (sem, val)
        for k, (sem, val) in need.items():
            self.seen[e][k] = val
        return list(need.values())

    @staticmethod
    def _deps(reads, writes, e=None):
        toks = [b.w for b in reads]
        for b in reads:
            if b.excl:
                toks.extend(t for t in b.r.values() if t[2] != e)
        for b in writes:
            toks.append(b.w)
            toks.extend(b.r.values())
        return toks

    @staticmethod
    def _update(tok, reads, writes):
        for b in writes:
            b.w = tok
            b.r = {}
        k = id(tok[0])
        for b in reads:
            o = b.r.get(k)
            if o is None or o[1] < tok[1]:
                b.r[k] = tok

    def mark(self, label):
        self.marks.append((label, self.seq, {e: len(self.prog[e]) for e in ENGS}))

    def op(self, e, fn, reads=(), writes=(), signal=True):
        self.seq += 1
        if self.limit and self.seq > self.limit:
            return
        waits = self._waits(e, self._deps(reads, writes, e))
        if signal:
            self.cnt[e] += 1
            tok = (self.sem[e], self.cnt[e], e)
            inc = (self.sem[e], 1)
        else:
            tok = (self.sem[e], self.cnt[e] + 1, e)
            inc = None
        self.prog[e].append((waits, fn, inc))
        self._update(tok, reads, writes)

    def dma(self, q, fn, reads=(), writes=()):
        self.seq += 1
        if self.limit and self.seq > self.limit:
            return
        i = self.rcnt[q] % self.K
        n = self.rcnt[q] // self.K
        self.rcnt[q] += 1
        sem = self.ring[q][i]
        toks = self._deps(reads, writes, q)
        if n > 0:
            toks.append((sem, 16 * n, "dma"))
        waits = self._waits(q, toks)
        tok = (sem, 16 * (n + 1), "dma")
        self.prog[q].append((waits, fn, (sem, 16)))
        self._update(tok, reads, writes)

    def finish(self):
        waits = []
        for q in ("sp", "pool"):
            for i in range(self.K):
                n = (self.rcnt[q] - i + self.K - 1) // self.K
                if n > 0:
                    waits.append((self.ring[q][i], 16 * n))
        self.prog["sp"].append((waits, None, None))

    def emit(self, block):
        def mk(e):
            def body(eng):
                for waits, fn, inc in self.prog[e]:
                    for sem, val in waits:
                        eng.wait_ge(sem, val)
                    if fn is None:
                        continue
                    ins = fn(eng)
                    if inc is not None:
                        ins.then_inc(*inc)
            return body
        block.tensor(mk("pe"))
        block.scalar(mk("act"))
        block.vector(mk("dve"))
        block.gpsimd(mk("pool"))
        block.sync(mk("sp"))


class Rot:
    def __init__(self, nc, es, name, n, width, dt):
        self.t = es.enter_context(nc.sbuf_tensor(name, [128, n, width], dt))
        self.b = [Buf() for _ in range(n)]
        self.n = n
        self.i = 0

    def next(self):
        i = self.i
        self.i = (i + 1) % self.n
        return self.t[:, i, :], self.b[i]


def build_program():
    nc = bass.Bass("TRN2", target_bir_lowering=False)
    es = ExitStack()
    with es:
        def din(name, shape, dt=F32):
            return nc.dram_tensor(name, shape, dt, kind="ExternalInput").ap()

        def dout(name, shape):
            return nc.dram_tensor(name, shape, F32, kind="ExternalOutput").ap()

        xp = din("xp", [SEQ, D]); meta = din("meta", [NMETA, D]); xs = din("xs", [32, D])
        ck = din("ck", [NPHYS * 128, 512]); cv = din("cv", [NPHYS * 128, 512])
        pt = din("pt", [4, PAGES], I32)
        spool = din("spool", [60, 512]); slc = din("slc", [12, D]); slh = din("slh", [4, D])
        sfc = din("sfc", [2, 8, FFN])
        w_in0 = din("w_in0", [D, 2048]); w_out0 = din("w_out0", [D, D]); poolw = din("poolw", [512, 128])
        lw_in = din("lw_in", [D, 2048]); lwa = din("lwa", [D, 128]); lwx = din("lwx", [D, 128])
        lw_out = din("lw_out", [D, D])
        fup = [din("fup0", [D, 2 * FFN]), din("fup1", [D, 2 * FFN])]
        fdn = [din("fdn0", [FFN, D]), din("fdn1", [FFN, D])]
        vec = din("vec", [128, VR]); hbr = din("hbr", [1, 384])

        yp = dout("yp", [SEQ, D]); ys = dout("ys", [32, D])
        kp = dout("kp", [L, 512]); vp = dout("vp", [L, 512]); ks = dout("ks", [32, 512]); vs = dout("vs", [32, 512])
        poolp = dout("poolp", [15, 512]); pools = dout("pools", [60, 512])
        lcp = dout("lcp", [3, D]); lcs = dout("lcs", [12, D]); lhp = dout("lhp", [1, D]); lhs = dout("lhs", [4, D])
        fcp = dout("fcp", [2, 2, FFN]); fcs = dout("fcs", [2, 8, FFN])

        S = Sched(nc, es)

        def sb(name, shape, dt=F32):
            return es.enter_context(nc.sbuf_tensor(name, shape, dt))

        identF = sb("identF", [128, 128]); identB = sb("identB", [128, 128], BF16)
        negtri = sb("negtri", [128, 128], BF16); negones = sb("negones", [128, 128], BF16)
        onesB = sb("onesB", [128, 128], BF16); meanm = sb("meanm", [128, 128], BF16)
        zerosB = sb("zerosB", [128, 128], BF16); maskM = sb("maskM", [128, 128], BF16)
        maskQ8 = sb("maskQ8", [128, 256], BF16); maskQ16 = sb("maskQ16", [128, 128], BF16)
        onesF = sb("onesF", [128, 128])
        CB = Buf()
        VEC = sb("VEC", [128, VR]); VB = Buf()
        SP8 = sb("SP8", [128, 8]); SPT = sb("SPT", [128, 8]); SP4 = sb("SP4", [128, 8])
        RC = sb("RC", [128, 4, 16]); RCI = sb("RCI", [128, 16], I32)
        HBH = sb("HBH", [1, 384], BF16); HBL = sb("HBL", [1, 384], BF16)
        PW = sb("PW", [128, 4, 128], BF16); PWB = Buf()
        LWA = sb("LWA", [128, 8, 128], BF16); LWX = sb("LWX", [128, 8, 128], BF16); LWB = Buf()

        xin = sb("xin", [128, 8, 512]); xinb = [Buf() for _ in range(8)]
        xT = sb("xT", [128, 8, NT]); xTb = sb("xTb", [128, 8, NT], BF16)
        xTB = [Buf() for _ in range(8)]; xTbB = [Buf() for _ in range(8)]
        KT = sb("KT", [128, 4, L], BF16); KTB = [Buf() for _ in range(4)]
        V = sb("V", [128, NBLK, 512], BF16); VBl = [Buf() for _ in range(NBLK)]
        QP = sb("QP", [128, 8, NT], BF16); QPB = [Buf() for _ in range(8)]
        KTS = sb("KTS", [128, 4, 32], BF16); KTSB = Buf()
        VSN = sb("VSN", [8, 4, 512], BF16); VSNB = [Buf() for _ in range(4)]
        NW = 10
        WR = sb("WR", [128, NW, 512], BF16); WRB = [Buf() for _ in range(NW)]
        wr_i = [0]
        gg = sb("gg", [128, NFC, NT], BF16); ggB = [Buf() for _ in range(NFC)]
        RF = Rot(nc, es, "RF", 9, 528, F32)
        RB = Rot(nc, es, "RB", 8, 512, BF16)
        ACC = sb("ACC", [128, 2, 512]); ACCB = sb("ACCB", [128, 2, 512], BF16)
        ACCb = [Buf(), Buf()]; ACCBb = [Buf(), Buf()]
        MEAN = sb("MEAN", [128, NT]); RSTD = sb("RSTD", [128, NT]); MR = sb("MR", [128, NT])
        MEANb, RSTDb, MRb = Buf(), Buf(), Buf()
        UHALO = sb("UHALO", [128, 4, 15]); RHALO = sb("RHALO", [128, 8, 3]); GHALO = sb("GHALO", [128, 2 * NFC, 2])
        UHb = [Buf() for _ in range(4)]; RHb = [Buf() for _ in range(8)]; GHb = [Buf() for _ in range(2 * NFC)]
        HST = sb("HST", [128, 8]); HSTb = [Buf() for _ in range(8)]
        HL = sb("HL", [128, 8, 5]); HLb = Buf()
        SU = sb("SU", [128, 4, 60]); SR = sb("SR", [128, 8, 12]); SG = sb("SG", [128, 2 * NFC, 8]); SH = sb("SH", [128, 8, 4])
        SSb = Buf()
        PTB = sb("PTB", [128, 1, 512], I32); PI = sb("PI", [128, 1], I32); PIF = sb("PIF", [128, 1])
        IDX = sb("IDX", [128, 512], U32); IDXb = Buf()

        ps = [es.enter_context(nc.psum_tensor("ps%d" % i, [128, 512], F32)) for i in range(8)]
        psb = [Buf(excl=True) for _ in range(8)]

        def cat(i):
            return gg[:, 8 + i, :]
        catB = ggB[8:16]

        def MM(out, lhsT, rhs, start, stop, reads, writes, signal=True):
            S.op("pe", lambda e: e.matmul(out, lhsT=lhsT, rhs=rhs, start=start, stop=stop), reads, writes, signal)

        def TR(out, in_, ident, reads, writes):
            S.op("pe", lambda e: e.transpose(out, in_, ident), reads, writes)

        def ACTF(out, in_, func, reads, writes, bias=None, scale=None):
            kw = {}
            if bias is not None:
                kw["bias"] = bias
            if scale is not None:
                kw["scale"] = scale
            S.op("act", lambda e: e.activation(out, in_, func, **kw), reads, writes)

        def CP(eng, out, in_, reads, writes):
            if eng == "act":
                S.op("act", lambda e: e.activation(out, in_, AF.Copy), reads, writes)
            else:
                S.op(eng, lambda e: e.tensor_copy(out=out, in_=in_), reads, writes)

        def TS(out, in0, s1, s2, op0, op1, reads, writes, eng="dve"):
            if op1 is None:
                S.op(eng, lambda e: e.tensor_scalar(out=out, in0=in0, scalar1=s1, scalar2=None, op0=op0), reads, writes)
            else:
                S.op(eng, lambda e: e.tensor_scalar(out=out, in0=in0, scalar1=s1, scalar2=s2, op0=op0, op1=op1), reads, writes)

        def STT(out, in0, scalar, in1, op0, op1, reads, writes):
            S.op("dve", lambda e: e.scalar_tensor_tensor(out=out, in0=in0, scalar=scalar, in1=in1, op0=op0, op1=op1), reads, writes)

        def TT(out, in0, in1, op, reads, writes, eng="dve"):
            S.op(eng, lambda e: e.tensor_tensor(out=out, in0=in0, in1=in1, op=op), reads, writes)

        def MEMSET(eng, ap, val, writes):
            S.op(eng, lambda e: e.memset(ap, val), (), writes)

        def DMA(q, out, in_, reads, writes):
            S.dma(q, lambda e: e.dma_start(out=out, in_=in_), reads, writes)

        def vcol(off):
            return VEC[:, off:off + 1]

        MEMSET("pool", onesF[:], 1.0, [CB]); MEMSET("pool", onesB[:], 1.0, [CB])
        MEMSET("pool", negones[:], -1.0, [CB]); MEMSET("pool", meanm[:], 1.0 / 1024.0, [CB])
        MEMSET("pool", zerosB[:], 0.0, [CB])

        def ASEL(out, in_, pattern, op, fill, cm):
            S.op("pool", lambda e: e.affine_select(out=out, in_=in_, pattern=pattern, compare_op=op, fill=fill,
                                                   base=0, channel_multiplier=cm), [CB], [CB])
        ASEL(identF[:], onesF[:], [[-1, 128]], ALU.is_equal, 0.0, 1)
        ASEL(identB[:], onesB[:], [[-1, 128]], ALU.is_equal, 0.0, 1)
        ASEL(negtri[:], negones[:], [[-1, 128]], ALU.is_ge, 0.0, 1)
        ASEL(maskM[:], zerosB[:], [[1, 128]], ALU.is_gt, NEG, -1)
        ASEL(maskQ8[:, 0:128], zerosB[:], [[0, 16], [1, 8]], ALU.is_gt, NEG, -1)
        ASEL(maskQ8[:, 128:256], zerosB[:], [[0, 16], [1, 8]], ALU.is_gt, NEG, -1)
        ASEL(maskQ16[:], zerosB[:], [[0, 8], [1, 16]], ALU.is_gt, NEG, -1)

        DMA("sp", VEC[:], vec, [], [VB])
        _hf, _hfb = RF.next()
        _ht, _htb = RF.next()
        HBF, HBT = _hf[0:1, 0:384], _ht[0:1, 0:384]
        DMA("sp", HBF, hbr, [], [_hfb])
        CP("dve", HBH[:], HBF, [_hfb], [VB])
        CP("dve", HBT, HBH[:], [VB], [_htb])
        TT(HBT, HBF, HBT, ALU.subtract, [_hfb, _htb], [_htb])
        CP("dve", HBL[:], HBT, [_htb], [VB])
        MEMSET("dve", QP[:], 0.0, QPB)
        ACTF(SPT[:], VEC[:, O_LAM:O_LAM + 8], AF.Exp, [VB], [VB], scale=-1.0)
        ACTF(SPT[:], SPT[:], AF.Ln, [VB], [VB], bias=1.0)
        TS(SP8[:], SPT[:], -8.0, None, ALU.mult, None, [VB], [VB])
        TS(SP4[:], SPT[:], -4.0, None, ALU.mult, None, [VB], [VB])
        S.op("pool", lambda e: e.iota(RCI[:], pattern=[[1, 16]], base=1, channel_multiplier=0), [], [VB])
        for c in range(4):
            CP("dve", RC[:, c, :], RCI[:], [VB], [VB])
            TS(RC[:, c, :], RC[:, c, :], float(2 << c), None, ALU.min, None, [VB], [VB])
            S.op("dve", (lambda cc: (lambda e: e.reciprocal(out=RC[:, cc, :], in_=RC[:, cc, :])))(c), [VB], [VB])
        DMA("pool", PW[:], poolw.rearrange("(g c) d -> c g d", c=128), [], [PWB])
        DMA("pool", LWA[:], lwa.rearrange("(n c) d -> c n d", c=128), [], [LWB])
        DMA("pool", LWX[:], lwx.rearrange("(n c) d -> c n d", c=128), [], [LWB])
        MEMSET("dve", UHALO[:], 0.0, UHb); MEMSET("dve", RHALO[:], 0.0, RHb)
        MEMSET("dve", GHALO[:], 0.0, GHb); MEMSET("dve", HST[:], 0.0, HSTb)
        DMA("sp", PTB[:], pt.rearrange("(o s) j -> o (s j)", o=1).partition_broadcast(128), [], [IDXb])
        S.op("pool", lambda e: e.iota(PI[:], pattern=[[0, 1]], base=0, channel_multiplier=1), [], [IDXb])
        CP("dve", PIF[:], PI[:], [IDXb], [IDXb])
        TS(IDX[:], PTB[:, 0, :], 128.0, PIF[:, 0:1], ALU.mult, ALU.add, [IDXb], [IDXb])

        def load_units(W, K, c0, width=512):
            return {"W": W, "K": K, "c0": c0, "width": width, "slots": {}}

        def unit_slot(u, kc):
            if kc not in u["slots"]:
                s = wr_i[0]
                wr_i[0] = (s + 1) % NW
                DMA("pool", WR[:, s, :u["width"]], u["W"][128 * kc:128 * kc + 128, u["c0"]:u["c0"] + u["width"]], [], [WRB[s]])
                u["slots"][kc] = s
            return u["slots"][kc]

        def fm_pass(units, rhs, n, banks, width=512):
            K = units["K"]
            noc = width // 128
            for kc in range(K):
                s = unit_slot(units, kc)
                rap, rb = rhs(kc)
                for oc in range(noc):
                    MM(ps[banks[oc]][:, :n], WR[:, s, 128 * oc:128 * oc + 128], rap[:, :n], kc == 0, kc == K - 1,
                       [WRB[s], rb], [psb[banks[oc]]], signal=(oc == noc - 1))

        def tm_pass(units, segs, banks, width=512):
            K = units["K"]
            for kc in range(K):
                s = unit_slot(units, kc)
                for si, (c0, m) in enumerate(segs):
                    MM(ps[banks[si]][:m, :width], xTb[:, kc, c0:c0 + m], WR[:, s, :width], kc == 0, kc == K - 1,
                       [WRB[s], xTbB[kc]], [psb[banks[si]]], signal=(si == len(segs) - 1))

        def xrhs(kc):
            return xTb[:, kc, :], xTbB[kc]

        def residual_ln(W, K, rhs, n, layer, which):
            for half in range(2):
                banks = [0, 1, 2, 3] if half == 0 else [4, 5, 6, 7]
                units = load_units(W, K, 512 * half)
                fm_pass(units, rhs, n, banks)
                for oc in range(4):
                    kc = 4 * half + oc
                    STT(xT[:, kc, :n], xT[:, kc, :n], DN_ALPHA, ps[banks[oc]][:, :n], ALU.mult, ALU.add,
                        [psb[banks[oc]], xTB[kc]], [xTB[kc]])
            for kc in range(8):
                tb, tbb = RB.next()
                CP("act", tb[:, :n], xT[:, kc, :n], [xTB[kc]], [tbb])
                sq, sqb = RB.next()
                TT(sq[:, :n], xT[:, kc, :n], xT[:, kc, :n], ALU.mult, [xTB[kc]], [sqb])
                MM(ps[0][:, :n], meanm[:], tb[:, :n], kc == 0, kc == 7, [CB, tbb], [psb[0]])
                MM(ps[1][:, :n], meanm[:], sq[:, :n], kc == 0, kc == 7, [CB, sqb], [psb[1]])
            CP("act", MEAN[:, :n], ps[0][:, :n], [psb[0]], [MEANb])
            TT(MR[:, :n], MEAN[:, :n], MEAN[:, :n], ALU.mult, [MEANb], [MRb])
            TT(RSTD[:, :n], ps[1][:, :n], MR[:, :n], ALU.subtract, [psb[1], MRb], [RSTDb])
            ACTF(RSTD[:, :n], RSTD[:, :n], AF.Sqrt, [RSTDb], [RSTDb], bias=LN_EPS)
            S.op("dve", lambda e: e.reciprocal(out=RSTD[:, :n], in_=RSTD[:, :n]), [RSTDb], [RSTDb])
            TT(MR[:, :n], MEAN[:, :n], RSTD[:, :n], ALU.mult, [MEANb, RSTDb], [MRb])
            gi = O_LNG + (layer * 2 + which) * 8
            bi = O_LNB + (layer * 2 + which) * 8
            for kc in range(8):
                t, tb_ = RF.next()
                TT(t[:, :n], xT[:, kc, :n], RSTD[:, :n], ALU.mult, [xTB[kc], RSTDb], [tb_])
                TT(t[:, :n], t[:, :n], MR[:, :n], ALU.subtract, [tb_, MRb], [tb_])
                TS(xT[:, kc, :n], t[:, :n], vcol(gi + kc), vcol(bi + kc), ALU.mult, ALU.add, [tb_, VB], [xTB[kc]])
                CP("act", xTb[:, kc, :n], xT[:, kc, :n], [xTB[kc]], [xTbB[kc]])

        def seg_layout(segs, H):
            out = []
            for si, (c0, m, kind) in enumerate(segs):
                base = 0 if si == 0 else (H + 16) + (si - 1) * (H + 8)
                out.append((base, c0, m, kind))
            tot = out[-1][0] + H + out[-1][2]
            return out, tot

        def halo_evac(bank, bankb, segs, H, halo_ap, halo_b, samp_ap, update=True):
            lay, tot = seg_layout(segs, H)
            w, wb = RF.next()
            for si, (base, c0, m, kind) in enumerate(lay):
                if kind == "p":
                    CP("dve", w[:, base:base + H], halo_ap, [halo_b], [wb])
                else:
                    CP("dve", w[:, base:base + H], samp_ap(int(kind[1])), [SSb], [wb])
                CP("act", w[:, base + H:base + H + m], bank[:, c0:c0 + m], [bankb], [wb])
            if update and len(segs) == 1:
                m = segs[0][1]
                CP("dve", halo_ap, w[:, m:m + H], [wb], [halo_b])
            return w, wb, lay, tot

        def ffn(layer, n, segs, tail):
            W = fup[layer]
            for cg in range(11):
                banks = [0, 1, 2, 3] if cg % 2 == 0 else [4, 5, 6, 7]
                units = load_units(W, 8, 512 * cg)
                fm_pass(units, xrhs, n, banks)
                ev = {}
                for i in range(4):
                    ch = 4 * cg + i
                    if ch < NFC:
                        hi = layer * NFC + ch
                        ev[i] = halo_evac(ps[banks[i]], psb[banks[i]], segs, 2, GHALO[:, hi, :], GHb[hi],
                                          lambda s, hi=hi: SG[:, hi, 2 * s:2 * s + 2])
                for i in range(4):
                    ch = 4 * cg + i
                    bk, bkb = ps[banks[i]], psb[banks[i]]
                    if ch < NFC:
                        w, wb, lay, tot = ev[i]
                        wo = O_FCW + layer * 66 + ch
                        t, tb_ = RF.next()
                        TS(t[:, 2:tot], w[:, 2:tot], vcol(wo + 44), vcol(O_FCB + layer * NFC + ch), ALU.mult, ALU.add,
                           [wb, VB], [tb_])
                        STT(t[:, 2:tot], w[:, 1:tot - 1], vcol(wo + 22), t[:, 2:tot], ALU.mult, ALU.add, [wb, tb_, VB], [tb_])
                        g2, g2b = RF.next()
                        for (base, c0, m, kind) in lay:
                            STT(g2[:, c0:c0 + m], w[:, base:base + m], vcol(wo), t[:, base + 2:base + 2 + m],
                                ALU.mult, ALU.add, [wb, tb_, VB], [g2b])
                        ACTF(gg[:, ch, :n], g2[:, :n], AF.Gelu, [g2b], [ggB[ch]])
                    else:
                        c2 = ch - NFC
                        TT(gg[:, c2, :n], gg[:, c2, :n], bk[:, :n], ALU.mult, [ggB[c2], bkb], [ggB[c2]])
                if tail and cg < 6:
                    tsegs = [(c0, m) for (c0, m, k) in segs]
                    tbk = [4, 5, 6, 7, 3] if cg % 2 == 0 else [0, 1, 2, 3, 7]
                    width = 512 if cg < 5 else 256
                    tm_pass(units, tsegs, tbk, width)
                    for si, (c0, m) in enumerate(tsegs):
                        st, stb = RF.next()
                        CP("act", st[:m, :width], ps[tbk[si]][:m, :width], [psb[tbk[si]]], [stb])
                        if si == 0:
                            DMA("sp", fcp[layer, :, 512 * cg:512 * cg + width], st[m - 2:m, :width], [stb], [])
                        else:
                            DMA("sp", fcs[layer, 2 * (si - 1):2 * si, 512 * cg:512 * cg + width], st[m - 2:m, :width], [stb], [])
            residual_ln(fdn[layer], NFC, lambda kc: (gg[:, kc, :], ggB[kc]), n, layer, 1)

        def attn_main(g, n):
            jmax = 4 * g + 3
            zr = [0]
            for h in range(8):
                p, r0 = h // 2, 64 * (h % 2)
                ob = 4 + (h % 2)
                a_, ab_ = h % 2, h % 2
                MM(ps[ob][:, :n], zerosB[:], QP[:, h, :n], True, False, [CB, QPB[h]], [psb[ob]])
                MEMSET("dve", ACC[:, a_, :], 0.0, [ACCb[a_]])
                hbias = vcol(O_HB + h)
                st = {}

                def stage1a(j):
                    z = zr[0]
                    zr[0] = (z + 1) % 4
                    c0 = max(0, 128 * (j - 4 * g))
                    diag = j >= 4 * g
                    MM(ps[z][:, c0:n], KT[:, p, 128 * j:128 * j + 128], QP[:, h, c0:n], True, False,
                       [KTB[p], QPB[h]], [psb[z]], signal=not diag)
                    if diag:
                        MM(ps[z][:, c0:c0 + 128], identB[:], maskM[:], False, False, [CB], [psb[z]])
                    st[j] = [z, c0, diag, None, None]

                def stage1b(j):
                    z, c0, diag, _, _ = st[j]
                    E, Eb = RF.next()
                    ACTF(E[:, c0:n], ps[z][:, c0:n], AF.Exp, [psb[z], VB], [Eb], bias=hbias)
                    Lp, Lpb = RB.next()
                    ACTF(Lp[:, c0:n], E[:, c0:n], AF.Ln, [Eb], [Lpb], bias=1.0)
                    st[j][3], st[j][4] = Lp, Lpb
                    if j > 0:
                        TT(ACC[:, a_, c0:n], ACC[:, a_, c0:n], Lp[:, c0:n], ALU.add, [ACCb[a_], Lpb], [ACCb[a_]])

                def stage2a(j):
                    z, c0, diag, Lp, Lpb = st[j]
                    cc0 = c0 + 128 if diag else 0
                    carry = (j != jmax) and cc0 < n
                    MM(ps[z][:, c0:n], negtri[:], Lp[:, c0:n], False, not carry, [CB, Lpb], [psb[z]], signal=not carry)
                    if carry:
                        MM(ps[z][:, cc0:n], negones[:], ACCB[:, ab_, cc0:n], False, True, [CB, ACCBb[ab_]], [psb[z]])
                    if j > 0:
                        CP("dve", ACCB[:, ab_, c0:n], ACC[:, a_, c0:n], [ACCb[a_]], [ACCBb[ab_]])

                def stage2b(j):
                    z, c0, diag, Lp, Lpb = st[j]
                    A, Ab = RB.next()
                    ACTF(A[:, c0:n], ps[z][:, c0:n], AF.Exp, [psb[z], VB], [Ab], bias=hbias)
                    st[j] += [A, Ab]

                def stage2c(j):
                    z, c0, diag, Lp, Lpb, A, Ab = st.pop(j)
                    MM(ps[ob][:, c0:n], V[:, j, 128 * p:128 * p + 128], A[:, c0:n], False, j == 0, [VBl[j], Ab], [psb[ob]])

                stage1a(jmax)
                if jmax >= 1:
                    stage1a(jmax - 1)
                stage1b(jmax)
                for j in range(jmax, -1, -1):
                    stage2a(j)
                    if j < jmax:
                        stage2c(j + 1)
                    if j >= 2:
                        stage1a(j - 2)
                    if j >= 1:
                        stage1b(j - 1)
                    stage2b(j)
                stage2c(0)
                CP("dve", cat(p)[r0:r0 + 64, :n], ps[ob][r0:r0 + 64, :n], [psb[ob]], [catB[p]])

        def attn_batched(items, nq, nblocks, get_block, maskQ, hb0):
            ni = len(items)
            Wd = ni * 8 * nq
            ob = 3
            MM(ps[ob][:, :Wd], zerosB[:], maskQ[:, :Wd], True, False, [CB], [psb[ob]])
            MEMSET("dve", ACC[:, 0, :], 0.0, [ACCb[0]])
            zr = [0]
            st = {}

            def stage1(j):
                blk, nk, diag = get_block(j)
                z = zr[0]
                zr[0] = (z + 1) % 3
                first = True
                for it, (qc0, oc0) in enumerate(items):
                    kt, ktb, v_ap, vb = blk[it]
                    for p in range(4):
                        col = (it * 8 + 2 * p) * nq
                        MM(ps[z][:nk, col:col + 2 * nq].rearrange("k (h q) -> k h q", h=2), kt(p)[:, :nk],
                           QP[:, 2 * p:2 * p + 2, qc0:qc0 + nq], first, False,
                           list(ktb) + [QPB[2 * p], QPB[2 * p + 1]], [psb[z]], signal=False)
                        first = False
                MM(ps[z][:nk, :Wd], onesB[0:1, :nk], HBH[0:1, hb0:hb0 + Wd], False, False, [CB, VB], [psb[z]], signal=False)
                MM(ps[z][:nk, :Wd], onesB[0:1, :nk], HBL[0:1, hb0:hb0 + Wd], False, False, [CB, VB], [psb[z]], signal=not diag)
                if diag:
                    MM(ps[z][:nk, :Wd], identB[:nk, :nk], maskQ[:nk, :Wd], False, False, [CB], [psb[z]])
                E, Eb = RF.next()
                ACTF(E[:nk, :Wd], ps[z][:nk, :Wd], AF.Exp, [psb[z]], [Eb])
                Lp, Lpb = RB.next()
                ACTF(Lp[:nk, :Wd], E[:nk, :Wd], AF.Ln, [Eb], [Lpb], bias=1.0)
                st[j] = (z, blk, nk, Lp, Lpb)

            def stage2(j):
                z, blk, nk, Lp, Lpb = st.pop(j)
                carry = j != nblocks - 1
                MM(ps[z][:nk, :Wd], negtri[:nk, :nk], Lp[:nk, :Wd], False, not carry, [CB, Lpb], [psb[z]], signal=not carry)
                if carry:
                    MM(ps[z][:nk, :Wd], negones[:, :nk], ACCB[:, 0, :Wd], False, True, [CB, ACCBb[0]], [psb[z]])
                A, Ab = RB.next()
                ACTF(A[:nk, :Wd], ps[z][:nk, :Wd], AF.Exp, [psb[z]], [Ab])
                cnt = 0
                for it in range(ni):
                    kt, ktb, v_ap, vb = blk[it]
                    for p in range(4):
                        col = (it * 8 + 2 * p) * nq
                        cnt += 1
                        MM(ps[ob][:, col:col + 2 * nq], v_ap[:nk, 128 * p:128 * p + 128], A[:nk, col:col + 2 * nq], False,
                           (j == 0 and cnt == ni * 4), list(vb) + [Ab], [psb[ob]], signal=(cnt == ni * 4))
                if j > 0:
                    TT(ACC[:nk, 0, :Wd], ACC[:nk, 0, :Wd], Lp[:nk, :Wd], ALU.add, [ACCb[0], Lpb], [ACCb[0]])
                    CP("dve", ACCB[:, 0, :Wd], ACC[:, 0, :Wd], [ACCb[0]], [ACCBb[0]])

            stage1(nblocks - 1)
            for j in range(nblocks - 1, -1, -1):
                if j > 0:
                    stage1(j - 1)
                stage2(j)
            for it, (qc0, oc0) in enumerate(items):
                for h in range(8):
                    p, r0 = h // 2, 64 * (h % 2)
                    col = (it * 8 + h) * nq
                    CP("dve", cat(p)[r0:r0 + 64, oc0:oc0 + nq], ps[ob][r0:r0 + 64, col:col + nq], [psb[ob]], [catB[p]])

        def load_sample_states():
            def tr_rows(src, rows, nchunks, dst_fn):
                w = nchunks * 128
                view = xin[:rows, :, :].rearrange("p a b -> p (a b)")
                DMA("sp", view[:, :w], src, [], xinb)
                for c in range(nchunks):
                    bk = c % 8
                    TR(ps[bk][:, :rows], view[:, 128 * c:128 * c + 128], identF[:rows, :rows], xinb + [CB], [psb[bk]])
                    CP("dve", dst_fn(c), ps[bk][:, :rows], [psb[bk]], [SSb])
            tr_rows(spool, 60, 4, lambda c: SU[:, c, :])
            tr_rows(slc, 12, 8, lambda c: SR[:, c, :])
            tr_rows(slh, 4, 8, lambda c: SH[:, c, :])
            for l in range(2):
                tr_rows(sfc[l], 8, NFC, lambda c, l=l: SG[:, l * NFC + c, :])
            for s in range(4):
                t, tb_ = RF.next()
                DMA("sp", t[:7, :512], spool[15 * s + 8:15 * s + 15, :], [], [tb_])
                DMA("sp", pools[15 * s:15 * s + 7, :], t[:7, :512], [tb_], [])

        groups = [(512, False, 512 * g) for g in range(NG)] + [(48, True, 4096)]
        S.mark('consts')
        load_sample_states()
        S.mark('states')
        for gi_, (n, tail, pos0) in enumerate(groups):
            if not tail:
                segs = [(0, 512, "p")]
                tsegs = [(128 * i, 128) for i in range(4)]
            else:
                segs = [(0, 16, "p")] + [(16 + 8 * s, 8, "s%d" % s) for s in range(4)]
                tsegs = [(c0, m) for (c0, m, k) in segs]
            npr = 512 if not tail else 16

            xv = xin[:, :, :].rearrange("p a b -> p (a b)")
            if not tail:
                for i in range(4):
                    rows0 = pos0 + 128 * i
                    dstv = xin[:, 2 * i:2 * i + 2, :].rearrange("p a b -> p (a b)")
                    if rows0 == 0:
                        DMA("sp", dstv[0:16, :], meta, [], xinb[0:2])
                        DMA("sp", dstv[16:128, :], xp[0:112, :], [], xinb[0:2])
                    else:
                        DMA("sp", dstv[:, :], xp[rows0 - 16:rows0 + 112, :], [], xinb[2 * i:2 * i + 2])
                for kc in range(8):
                    for i in range(4):
                        dstv = xin[:, 2 * i:2 * i + 2, :].rearrange("p a b -> p (a b)")
                        TR(ps[kc][:, 128 * i:128 * i + 128], dstv[:, 128 * kc:128 * kc + 128], identF[:], xinb[2 * i:2 * i + 2] + [CB], [psb[kc]])
            else:
                dstv = xin[:, 0:2, :].rearrange("p a b -> p (a b)")
                DMA("sp", dstv[0:16, :], xp[SEQ - 16:SEQ, :], [], xinb[0:2])
                DMA("sp", dstv[16:48, :], xs, [], xinb[0:2])
                for kc in range(8):
                    TR(ps[kc][:, 0:48], dstv[0:48, 128 * kc:128 * kc + 128], identF[:48, :48], xinb[0:2] + [CB], [psb[kc]])
            for kc in range(8):
                CP("act", xT[:, kc, :n], ps[kc][:, :n], [psb[kc]], [xTB[kc]])
                CP("dve", xTb[:, kc, :n], ps[kc][:, :n], [psb[kc]], [xTbB[kc]])

            S.mark('g%d:G0' % gi_)
            units = load_units(w_in0, 8, 0)
            fm_pass(units, xrhs, n, [0, 1, 2, 3])
            for h in range(8):
                r0 = 64 * (h % 2)
                ACTF(QP[r0:r0 + 64, h, :n], ps[h // 2][r0:r0 + 64, :n], AF.Copy, [psb[h // 2]], [QPB[h]], scale=0.125)
            units = load_units(w_in0, 8, 512)
            fm_pass(units, xrhs, n, [4, 5, 6, 7])
            for oc in range(4):
                CP("dve", KT[:, oc, pos0:pos0 + npr], ps[4 + oc][:, :npr], [psb[4 + oc]], [KTB[oc]])
                if tail:
                    CP("dve", KTS[:, oc, :], ps[4 + oc][:, 16:48], [psb[4 + oc]], [KTSB])
            kb = [0, 1, 2, 3] if not tail else [0, 1, 2, 3, 4]
            tm_pass(units, tsegs, kb)
            for si, (c0, m) in enumerate(tsegs):
                st, stb = RF.next()
                CP("act", st[:m, :512], ps[kb[si]][:m, :], [psb[kb[si]]], [stb])
                if not tail or si == 0:
                    DMA("sp", kp[pos0 + c0:pos0 + c0 + m, :], st[:m, :512], [stb], [])
                else:
                    DMA("sp", ks[8 * (si - 1):8 * si, :], st[:m, :512], [stb], [])
            units = load_units(w_in0, 8, 1024)
            vb_ = [4, 5, 6, 7] if not tail else [5, 6, 7, 3, 4]
            tm_pass(units, tsegs, vb_)
            for si, (c0, m) in enumerate(tsegs):
                st, stb = RF.next()
                CP("act", st[:m, :512], ps[vb_[si]][:m, :], [psb[vb_[si]]], [stb])
                if not tail or si == 0:
                    blk = (pos0 + c0) // 128
                    CP("dve", V[:m, blk, :], ps[vb_[si]][:m, :], [psb[vb_[si]]], [VBl[blk]])
                    DMA("sp", vp[pos0 + c0:pos0 + c0 + m, :], st[:m, :512], [stb], [])
                else:
                    CP("dve", VSN[:, si - 1, :], ps[vb_[si]][:m, :], [psb[vb_[si]]], [VSNB[si - 1]])
                    DMA("sp", vs[8 * (si - 1):8 * si, :], st[:m, :512], [stb], [])
            units = load_units(w_in0, 8, 1536)
            fm_pass(units, xrhs, n, [0, 1, 2, 3])
            for c in range(4):
                w, wb, lay, tot = halo_evac(ps[c], psb[c], segs, 15, UHALO[:, c, :], UHb[c],
                                            lambda s, c=c: SU[:, c, 15 * s:15 * s + 15])
                wn = 2 << c
                a, ab = RF.next()
                TT(a[:, 1:tot], w[:, 1:tot], w[:, 0:tot - 1], ALU.add, [wb], [ab])
                cur, curb, sh = a, ab, 2
                while sh < wn:
                    nb, nbb = RF.next()
                    lo = 2 * sh - 1
                    TT(nb[:, lo:tot], cur[:, lo:tot], cur[:, lo - sh:tot - sh], ALU.add, [curb], [nbb])
                    cur, curb, sh = nb, nbb, 2 * sh
                dd, ddb = RB.next()
                for (base, c0, m, kind) in lay:
                    o = base + 15
                    STT(dd[:, c0:c0 + m], cur[:, o:o + m], 1.0 / wn, w[:, o:o + m], ALU.mult, ALU.subtract, [curb, wb], [ddb])
                    if gi_ == 0:
                        t2, t2b = RF.next()
                        TT(t2[:, 0:16], cur[:, o:o + 16], RC[:, c, :], ALU.mult, [curb, VB], [t2b])
                        TT(dd[:, c0:c0 + 16], t2[:, 0:16], w[:, o:o + 16], ALU.subtract, [t2b, wb], [ddb])
                bk = 6 + (c % 2)
                MM(ps[bk][:, :n], PW[:, c, :], dd[:, :n], True, True, [PWB, ddb], [psb[bk]])
                TS(cat(4 + c)[:, :n], ps[bk][:, :n], vcol(O_PSC + c), None, ALU.mult, None, [psb[bk], VB], [catB[4 + c]])
            if tail:
                ub = [4, 5, 6, 7, 0]
                tm_pass(units, tsegs, ub)
                for si, (c0, m) in enumerate(tsegs):
                    st, stb = RF.next()
                    CP("act", st[:m, :512], ps[ub[si]][:m, :], [psb[ub[si]]], [stb])
                    if si == 0:
                        DMA("sp", poolp[:, :], st[1:16, :512], [stb], [])
                    else:
                        DMA("sp", pools[15 * (si - 1) + 7:15 * si, :], st[:8, :512], [stb], [])
            S.mark('g%d:inproj+pool' % gi_)
            if not tail:
                attn_main(gi_, n)
            else:
                def blk_prompt(j):
                    nk = 128 if j < 32 else 16
                    return ([(lambda p, j=j, nk=nk: KT[:, p, 128 * j:128 * j + nk], KTB, V[:, j, :], [VBl[j]])], nk, j == 32)
                attn_batched([(0, 0)], 16, NBLK, blk_prompt, maskQ16, 256)

                pg_state = {}

                def blk_sample(j):
                    if j == PAGES:
                        return ([(lambda p, s=s: KTS[:, p, 8 * s:8 * s + 8], [KTSB], VSN[:, s, :], [VSNB[s]]) for s in range(4)], 8, True)
                    res = []
                    for s in range(4):
                        col = s * 128 + j
                        kpg, kpb = xin[:, 2 * s, :], xinb[2 * s]
                        vpg, vpb = xin[:, 2 * s + 1, :], xinb[2 * s + 1]
                        S.dma("pool", (lambda kpg=kpg, col=col: (lambda e: e.indirect_dma_start(
                            out=kpg, out_offset=None, in_=ck, in_offset=bass.IndirectOffsetOnAxis(ap=IDX[:, col:col + 1], axis=0))))(),
                            [IDXb], [kpb])
                        S.dma("pool", (lambda vpg=vpg, col=col: (lambda e: e.indirect_dma_start(
                            out=vpg, out_offset=None, in_=cv, in_offset=bass.IndirectOffsetOnAxis(ap=IDX[:, col:col + 1], axis=0))))(),
                            [IDXb], [vpb])
                        tb = 4 + s
                        for p in range(4):
                            TR(ps[tb][:, 128 * p:128 * p + 128], kpg[:, 128 * p:128 * p + 128], identF[:], [kpb, CB], [psb[tb]])
                        vch = (4 + s) if (j % 2 == 0) else (16 + s)
                        CP("dve", gg[:, s, :], ps[tb][:, :], [psb[tb]], [ggB[s]])
                        CP("dve", gg[:, vch, :], vpg, [vpb], [ggB[vch]])
                        res.append((lambda p, s=s: gg[:, s, 128 * p:128 * p + 128], [ggB[s]], gg[:, vch, :], [ggB[vch]]))
                    return (res, 128, False)
                attn_batched([(16 + 8 * s, 16 + 8 * s) for s in range(4)], 8, PAGES + 1, blk_sample, maskQ8, 0)
            S.mark('g%d:attn' % gi_)
            residual_ln(w_out0, 8, lambda kc: (cat(kc), catB[kc]), n, 0, 0)
            S.mark('g%d:ln1' % gi_)
            ffn(0, n, segs, tail)
            S.mark('g%d:ffn0' % gi_)

            for cg in range(2):
                banks = [0, 1, 2, 3] if cg == 0 else [4, 5, 6, 7]
                units = load_units(lw_in, 8, 512 * cg)
                fm_pass(units, xrhs, n, banks)
                for i in range(4):
                    c = 4 * cg + i
                    ACTF(gg[:, c, :n], ps[banks[i]][:, :n], AF.Gelu, [psb[banks[i]]], [ggB[c]])
            for cg in range(2, 4):
                banks = [0, 1, 2, 3] if cg == 2 else [4, 5, 6, 7]
                units = load_units(lw_in, 8, 512 * cg)
                fm_pass(units, xrhs, n, banks)
                if tail:
                    tbk = [4, 5, 6, 7, 3] if cg == 2 else [0, 1, 2, 3, 7]
                    tm_hold = (units, tbk, cg)
                else:
                    tm_hold = None
                for i in range(4):
                    c = 4 * (cg - 2) + i
                    bk, bkb = ps[banks[i]], psb[banks[i]]
                    w, wb, lay, tot = halo_evac(bk, bkb, segs, 3, RHALO[:, c, :], RHb[c], lambda s, c=c: SR[:, c, 3 * s:3 * s + 3])
                    t, tb_ = RF.next()
                    TS(t[:, 3:tot], w[:, 3:tot], vcol(O_LCW + 24 + c), vcol(O_LCB + c), ALU.mult, ALU.add, [wb, VB], [tb_])
                    STT(t[:, 3:tot], w[:, 2:tot - 1], vcol(O_LCW + 16 + c), t[:, 3:tot], ALU.mult, ALU.add, [wb, tb_, VB], [tb_])
                    STT(t[:, 3:tot], w[:, 1:tot - 2], vcol(O_LCW + 8 + c), t[:, 3:tot], ALU.mult, ALU.add, [wb, tb_, VB], [tb_])
                    xc, xcb = RF.next()
                    for (base, c0, m, kind) in lay:
                        STT(xc[:, c0:c0 + m], w[:, base:base + m], vcol(O_LCW + c), t[:, base + 3:base + 3 + m],
                            ALU.mult, ALU.add, [wb, tb_, VB], [xcb])
                    xb16, xb16b = RB.next()
                    CP("act", xb16[:, :n], xc[:, :n], [xcb], [xb16b])
                    bnk = banks[i]
                    r, rb = RF.next()
                    MM(ps[bnk][:, :n], LWA[:, c, :], xb16[:, :n], True, True, [LWB, xb16b], [psb[bnk]])
                    ACTF(r[:, :n], ps[bnk][:, :n], AF.Sigmoid, [psb[bnk], VB], [rb], bias=vcol(O_LBA + c))
                    gi2, gib = RF.next()
                    MM(ps[bnk][:, :n], LWX[:, c, :], xb16[:, :n], True, True, [LWB, xb16b], [psb[bnk]])
                    ACTF(gi2[:, :n], ps[bnk][:, :n], AF.Sigmoid, [psb[bnk], VB], [gib], bias=vcol(O_LBX + c))
                    a, ab = RF.next()
                    ACTF(a[:, :n], r[:, :n], AF.Tanh, [rb, VB], [ab], scale=SP4[:, c:c + 1])
                    t1, t1b = RF.next()
                    TS(t1[:, :n], a[:, :n], -1.0, 1.0, ALU.mult, ALU.add, [ab], [t1b])
                    S.op("dve", (lambda t1=t1, n=n: (lambda e: e.reciprocal(out=t1[:, :n], in_=t1[:, :n])))(), [t1b], [t1b])
                    TS(a[:, :n], a[:, :n], 1.0, None, ALU.add, None, [ab], [ab])
                    TT(a[:, :n], a[:, :n], t1[:, :n], ALU.mult, [ab, t1b], [ab])
                    ACTF(r[:, :n], r[:, :n], AF.Tanh, [rb, VB], [rb], scale=SP8[:, c:c + 1])
                    TT(t1[:, :n], a[:, :n], a[:, :n], ALU.mult, [ab, t1b], [t1b])
                    STT(t1[:, :n], t1[:, :n], 1.0, r[:, :n], ALU.add, ALU.mult, [t1b, rb], [t1b])
                    ACTF(t1[:, :n], t1[:, :n], AF.Sqrt, [t1b], [t1b], scale=-1.0)
                    TT(gi2[:, :n], gi2[:, :n], xc[:, :n], ALU.mult, [gib, xcb], [gib])
                    TT(gi2[:, :n], gi2[:, :n], t1[:, :n], ALU.mult, [gib, t1b], [gib])
                    hh, hb_ = RF.next()
                    for si, (base, c0, m, kind) in enumerate(lay):
                        if kind == "p":
                            init, initb = HST[:, c:c + 1], HSTb[c]
                        else:
                            s_ = int(kind[1])
                            init, initb = SH[:, c, s_:s_ + 1], SSb
                        S.op("dve", (lambda c0=c0, m=m, init=init, hh=hh, a=a, gi2=gi2: (lambda e: e.tensor_tensor_scan(
                            out=hh[:, c0:c0 + m], data0=a[:, c0:c0 + m], data1=gi2[:, c0:c0 + m], initial=init,
                            op0=ALU.mult, op1=ALU.add)))(), [ab, gib, initb], [hb_])
                        if kind == "p":
                            CP("dve", HST[:, c:c + 1], hh[:, c0 + m - 1:c0 + m], [hb_], [HSTb[c]])
                        if tail:
                            CP("dve", HL[:, c, si:si + 1], hh[:, c0 + m - 1:c0 + m], [hb_], [HLb])
                    TT(cat(c)[:, :n], gg[:, c, :n], hh[:, :n], ALU.mult, [ggB[c], hb_], [catB[c]])
                if tm_hold is not None:
                    units_, tbk, cg_ = tm_hold
                    tm_pass(units_, tsegs, tbk)
                    for si, (c0, m) in enumerate(tsegs):
                        st, stb = RF.next()
                        CP("act", st[:m, :512], ps[tbk[si]][:m, :], [psb[tbk[si]]], [stb])
                        cc = 512 * (cg_ - 2)
                        if si == 0:
                            DMA("sp", lcp[:, cc:cc + 512], st[m - 3:m, :512], [stb], [])
                        else:
                            DMA("sp", lcs[3 * (si - 1):3 * si, cc:cc + 512], st[m - 3:m, :512], [stb], [])
            if tail:
                for c in range(8):
                    bk = 0 if c < 4 else 1
                    TR(ps[bk][:5, 128 * (c % 4):128 * (c % 4) + 128], HL[:, c, :], identF[:], [HLb, CB], [psb[bk]])
                for half in range(2):
                    st, stb = RF.next()
                    CP("act", st[:5, :512], ps[half][:5, :], [psb[half]], [stb])
                    DMA("sp", lhp[:, 512 * half:512 * half + 512], st[0:1, :512], [stb], [])
                    DMA("sp", lhs[:, 512 * half:512 * half + 512], st[1:5, :512], [stb], [])
            S.mark('g%d:lru' % gi_)
            residual_ln(lw_out, 8, lambda kc: (cat(kc), catB[kc]), n, 1, 0)
            S.mark('g%d:ln3' % gi_)
            ffn(1, n, segs, tail)
            S.mark('g%d:ffn1' % gi_)

            for si, (c0, m) in enumerate(tsegs if not tail else [(0, 48)]):
                b0 = 2 * (si % 4)
                for kc in range(8):
                    bk = b0 + (kc // 4)
                    TR(ps[bk][:m, 128 * (kc % 4):128 * (kc % 4) + 128], xT[:, kc, c0:c0 + m], identF[:], [xTB[kc], CB], [psb[bk]])
                for half in range(2):
                    st, stb = RF.next()
                    CP("act" if half == 0 else "dve", st[:m, :512], ps[b0 + half][:m, :], [psb[b0 + half]], [stb])
                    cs = slice(512 * half, 512 * half + 512)
                    if not tail:
                        r0_ = pos0 + c0 - 16
                        if r0_ < 0:
                            DMA("sp", yp[0:112, cs], st[16:128, :512], [stb], [])
                        else:
                            DMA("sp", yp[r0_:r0_ + 128, cs], st[:128, :512], [stb], [])
                    else:
                        DMA("sp", yp[SEQ - 16:SEQ, cs], st[0:16, :512], [stb], [])
                        DMA("sp", ys[:, cs], st[16:48, :512], [stb], [])

        S.finish()
        build_program.stats = {e: len(S.prog[e]) for e in ENGS}
        build_program.marks = S.marks
        with nc.Block() as block:
            S.emit(block)
    return nc


_NC_CACHE = {}


def kernel(**inp):
    f = lambda a: np.ascontiguousarray(np.asarray(a, dtype=np.float32))
    x_prompt = f(inp["x_prompt"]); x_sample = f(inp["x_sample"])
    ck = f(inp["cache_sb_k"]).reshape(NPHYS * 128, 512)
    cv = f(inp["cache_sb_v"]).reshape(NPHYS * 128, 512)
    page_table = np.ascontiguousarray(np.asarray(inp["page_table"], dtype=np.int32))

    def cm(v):
        return np.ascontiguousarray(f(v).reshape(-1, 128).T)
    hb = f(inp["sb_logit_bias"])[0]
    vec = np.concatenate([
        cm(inp["ln_g"]), cm(inp["ln_b"]), cm(inp["pool_scale"]), cm(inp["lru_conv_w"]), cm(inp["lru_conv_b"]),
        cm(inp["lru_b_a"]), cm(inp["lru_b_x"]), cm(inp["lru_lambda"]), cm(inp["ffn_conv_w"]), cm(inp["ffn_conv_b"]),
        np.tile(hb[None, :], (128, 1))], axis=1).astype(np.float32)
    assert vec.shape == (128, VR)
    hbr = np.concatenate([np.tile(np.repeat(hb, 8), 4), np.repeat(hb, 16)])[None, :].astype(np.float32)

    common = dict(
        meta=f(inp["meta_tokens"]), ck=ck, cv=cv,
        w_in0=f(inp["sb_w_in"])[0], w_out0=f(inp["sb_w_out"])[0], poolw=f(inp["pool_w"])[0].reshape(512, 128),
        lw_in=f(inp["lru_w_in"])[0], lwa=f(inp["lru_w_a"])[0].reshape(1024, 128), lwx=f(inp["lru_w_x"])[0].reshape(1024, 128),
        lw_out=f(inp["lru_w_out"])[0], fup0=f(inp["ffn_w_up"])[0], fup1=f(inp["ffn_w_up"])[1],
        fdn0=f(inp["ffn_w_down"])[0], fdn1=f(inp["ffn_w_down"])[1], vec=vec, hbr=hbr)
    in_maps = []
    for c in range(8):
        b = c % 4
        sl = slice(4 * c, 4 * c + 4)
        m = dict(common)
        m.update(
            xp=x_prompt[b], xs=np.ascontiguousarray(x_sample[sl].reshape(32, D)),
            pt=np.ascontiguousarray(page_table[sl]),
            spool=np.ascontiguousarray(f(inp["state_pool"])[0, sl].reshape(60, 512)),
            slc=np.ascontiguousarray(f(inp["state_lru_conv"])[0, sl].reshape(12, D)),
            slh=np.ascontiguousarray(f(inp["state_lru_h"])[0, sl]),
            sfc=np.ascontiguousarray(f(inp["state_ffn_conv"])[:, sl].reshape(2, 8, FFN)))
        in_maps.append(m)
    if "nc" not in _NC_CACHE:
        _NC_CACHE["nc"] = build_program()
    res = run_bass_kernel_spmd(_NC_CACHE["nc"], in_maps, core_ids=list(range(8)))
    R = res.results
    pr = lambda k: np.stack([R[b][k] for b in range(4)])
    sm = lambda k: np.concatenate([R[c][k] for c in range(8)], axis=0)
    y_prompt = pr("yp")
    y_sample = sm("ys").reshape(32, 8, D)
    k_prompt = pr("kp").reshape(1, 4, L, 8, 64); v_prompt = pr("vp").reshape(1, 4, L, 8, 64)
    k_sample = sm("ks").reshape(1, 32, 8, 8, 64); v_sample = sm("vs").reshape(1, 32, 8, 8, 64)
    pool_prompt = pr("poolp").reshape(1, 4, 15, 512); pool_sample = sm("pools").reshape(1, 32, 15, 512)
    lcp = pr("lcp").reshape(1, 4, 3, D); lcs = sm("lcs").reshape(1, 32, 3, D)
    lhp = pr("lhp").reshape(1, 4, D); lhs = sm("lhs").reshape(1, 32, D)
    fcp = np.stack([R[b]["fcp"] for b in range(4)], axis=1)
    fcs = np.concatenate([R[c]["fcs"].reshape(2, 4, 2, FFN) for c in range(8)], axis=1)
    outs = (y_prompt, y_sample, k_prompt, v_prompt, k_sample, v_sample, pool_prompt, pool_sample,
            lcp, lcs, lhp, lhs, fcp, fcs)
    return tuple(np.ascontiguousarray(o, dtype=np.float32) for o in outs)
```

```python
import numpy as np
from contextlib import ExitStack
import concourse.bass as bass
import concourse.mybir as mybir
from concourse.bass_utils import run_bass_kernel_spmd

F32 = mybir.dt.float32
BF16 = mybir.dt.bfloat16
I32 = mybir.dt.int32
U32 = mybir.dt.uint32
AF = mybir.ActivationFunctionType
ALU = mybir.AluOpType

D = 1024
SEQ = 4096
NMETA = 16
L = SEQ + NMETA
NT = 512
NG = 8
NBLK = 33
FFN = 2816
NFC = 22
PAGES = 128
NPHYS = 5120
DN_ALPHA = (2.0 * 2) ** 0.25
LN_EPS = 1e-5
NEG = -30000.0
LIMIT = 0

O_LNG, O_LNB, O_PSC, O_LCW, O_LCB, O_LBA, O_LBX, O_LAM, O_FCW, O_FCB, O_HB, VR = (
    0, 32, 64, 68, 100, 108, 116, 124, 132, 264, 308, 316)

ENGS = ("pe", "act", "dve", "pool", "sp")


class Buf:
    __slots__ = ("w", "r", "excl")

    def __init__(self, excl=False):
        self.w = None
        self.r = {}
        self.excl = excl


class Sched:
    K = 8

    def __init__(self, nc, es):
        self.nc = nc
        self.prog = {e: [] for e in ENGS}
        self.sem = {e: es.enter_context(nc.semaphore("s_" + e)) for e in ENGS}
        self.cnt = {e: 0 for e in ENGS}
        self.seen = {e: {} for e in ENGS}
        self.ring = {q: [es.enter_context(nc.semaphore("r_%s_%d" % (q, i))) for i in range(self.K)]
                     for q in ("sp", "pool")}
        self.rcnt = {q: 0 for q in ("sp", "pool")}
        self.seq = 0
        self.limit = LIMIT
        self.marks = []

    def _waits(self, e, toks):
        need = {}
        for t in toks:
            if t is None:
                continue
            sem, val, te = t
            if te == "pe" and e == "pe":
                continue
            k = id(sem)
            if self.seen[e].get(k, 0) >= val:
                continue
            if k not in need or need[k][1] < val:
                need[k] = (sem, val)
        for k, (sem, val) in need.items():
            self.seen[e][k] = val
        return list(need.values())

    @staticmethod
    def _deps(reads, writes, e=None):
        toks = [b.w for b in reads]
        for b in reads:
            if b.excl:
                toks.extend(t for t in b.r.values() if t[2] != e)
        for b in writes:
            toks.append(b.w)
            toks.extend(b.r.values())
        return toks

    @staticmethod
    def _update(tok, reads, writes):
        for b in writes:
            b.w = tok
            b.r = {}
        k = id(tok[0])
        for b in reads:
            o = b.r.get(k)
            if o is None or o[1] < tok[1]:
                b.r[k] = tok

    def mark(self, label):
        self.marks.append((label, self.seq, {e: len(self.prog[e]) for e in ENGS}))

    def op(self, e, fn, reads=(), writes=(), signal=True):
        self.seq += 1
        if self.limit and self.seq > self.limit:
            return
        waits = self._waits(e, self._deps(reads, writes, e))
        if signal:
            self.cnt[e] += 1
            tok = (self.sem[e], self.cnt[e], e)
            inc = (self.sem[e], 1)
        else:
            tok = (self.sem[e], self.cnt[e] + 1, e)
            inc = None
        self.prog[e].append((waits, fn, inc))
        self._update(tok, reads, writes)

    def dma(self, q, fn, reads=(), writes=()):
        self.seq += 1
        if self.limit and self.seq > self.limit:
            return
        i = self.rcnt[q] % self.K
        n = self.rcnt[q] // self.K
        self.rcnt[q] += 1
        sem = self.ring[q][i]
        toks = self._deps(reads, writes, q)
        if n > 0:
            toks.append((sem, 16 * n, "dma"))
        waits = self._waits(q, toks)
        tok = (sem, 16 * (n + 1), "dma")
        self.prog[q].append((waits, fn, (sem, 16)))
        self._update(tok, reads, writes)

    def finish(self):
        waits = []
        for q in ("sp", "pool"):
            for i in range(self.K):
                n = (self.rcnt[q] - i + self.K - 1) // self.K
                if n > 0:
                    waits.append((self.ring[q][i], 16 * n))
        self.prog["sp"].append((waits, None, None))

    def emit(self, block):
        def mk(e):
            def body(eng):
                for waits, fn, inc in self.prog[e]:
                    for sem, val in waits:
                        eng.wait_ge(sem, val)
                    if fn is None:
                        continue
                    ins = fn(eng)
                    if inc is not None:
                        ins.then_inc(*inc)
            return body
        block.tensor(mk("pe"))
        block.scalar(mk("act"))
        block.vector(mk("dve"))
        block.gpsimd(mk("pool"))
        block.sync(mk("sp"))


class Rot:
    def __init__(self, nc, es, name, n, width, dt):
        self.t = es.enter_context(nc.sbuf_tensor(name, [128, n, width], dt))
        self.b = [Buf() for _ in range(n)]
        self.n = n
        self.i = 0

    def next(self):
        i = self.i
        self.i = (i + 1) % self.n
        return self.t[:, i, :], self.b[i]


def build_program():
    nc = bass.Bass("TRN2", target_bir_lowering=False)
    es = ExitStack()
    with es:
        def din(name, shape, dt=F32):
            return nc.dram_tensor(name, shape, dt, kind="ExternalInput").ap()

        def dout(name, shape):
            return nc.dram_tensor(name, shape, F32, kind="ExternalOutput").ap()

        xp = din("xp", [SEQ, D]); meta = din("meta", [NMETA, D]); xs = din("xs", [32, D])
        ck = din("ck", [NPHYS * 128, 512]); cv = din("cv", [NPHYS * 128, 512])
        pt = din("pt", [4, PAGES], I32)
        spool = din("spool", [60, 512]); slc = din("slc", [12, D]); slh = din("slh", [4, D])
        sfc = din("sfc", [2, 8, FFN])
        w_in0 = din("w_in0", [D, 2048]); w_out0 = din("w_out0", [D, D]); poolw = din("poolw", [512, 128])
        lw_in = din("lw_in", [D, 2048]); lwa = din("lwa", [D, 128]); lwx = din("lwx", [D, 128])
        lw_out = din("lw_out", [D, D])
        fup = [din("fup0", [D, 2 * FFN]), din("fup1", [D, 2 * FFN])]
        fdn = [din("fdn0", [FFN, D]), din("fdn1", [FFN, D])]
        vec = din("vec", [128, VR]); hbr = din("hbr", [1, 384])

        yp = dout("yp", [SEQ, D]); ys = dout("ys", [32, D])
        kp = dout("kp", [L, 512]); vp = dout("vp", [L, 512]); ks = dout("ks", [32, 512]); vs = dout("vs", [32, 512])
        poolp = dout("poolp", [15, 512]); pools = dout("pools", [60, 512])
        lcp = dout("lcp", [3, D]); lcs = dout("lcs", [12, D]); lhp = dout("lhp", [1, D]); lhs = dout("lhs", [4, D])
        fcp = dout("fcp", [2, 2, FFN]); fcs = dout("fcs", [2, 8, FFN])

        S = Sched(nc, es)

        def sb(name, shape, dt=F32):
            return es.enter_context(nc.sbuf_tensor(name, shape, dt))

        identF = sb("identF", [128, 128]); identB = sb("identB", [128, 128], BF16)
        negtri = sb("negtri", [128, 128], BF16); negones = sb("negones", [128, 128], BF16)
        onesB = sb("onesB", [128, 128], BF16); meanm = sb("meanm", [128, 128], BF16)
        zerosB = sb("zerosB", [128, 128], BF16); maskM = sb("maskM", [128, 128], BF16)
        maskQ8 = sb("maskQ8", [128, 256], BF16); maskQ16 = sb("maskQ16", [128, 128], BF16)
        onesF = sb("onesF", [128, 128])
        CB = Buf()
        VEC = sb("VEC", [128, VR]); VB = Buf()
        SP8 = sb("SP8", [128, 8]); SPT = sb("SPT", [128, 8]); SP4 = sb("SP4", [128, 8])
        RC = sb("RC", [128, 4, 16]); RCI = sb("RCI", [128, 16], I32)
        HBH = sb("HBH", [1, 384], BF16); HBL = sb("HBL", [1, 384], BF16)
        PW = sb("PW", [128, 4, 128], BF16); PWB = Buf()
        LWA = sb("LWA", [128, 8, 128], BF16); LWX = sb("LWX", [128, 8, 128], BF16); LWB = Buf()

        xin = sb("xin", [128, 8, 512]); xinb = [Buf() for _ in range(8)]
        xT = sb("xT", [128, 8, NT]); xTb = sb("xTb", [128, 8, NT], BF16)
        xTB = [Buf() for _ in range(8)]; xTbB = [Buf() for _ in range(8)]
        KT = sb("KT", [128, 4, L], BF16); KTB = [Buf() for _ in range(4)]
        V = sb("V", [128, NBLK, 512], BF16); VBl = [Buf() for _ in range(NBLK)]
        QP = sb("QP", [128, 8, NT], BF16); QPB = [Buf() for _ in range(8)]
        KTS = sb("KTS", [128, 4, 32], BF16); KTSB = Buf()
        VSN = sb("VSN", [8, 4, 512], BF16); VSNB = [Buf() for _ in range(4)]
        NW = 10
        WR = sb("WR", [128, NW, 512], BF16); WRB = [Buf() for _ in range(NW)]
        wr_i = [0]
        gg = sb("gg", [128, NFC, NT], BF16); ggB = [Buf() for _ in range(NFC)]
        RF = Rot(nc, es, "RF", 9, 528, F32)
        RB = Rot(nc, es, "RB", 8, 512, BF16)
        ACC = sb("ACC", [128, 2, 512]); ACCB = sb("ACCB", [128, 2, 512], BF16)
        ACCb = [Buf(), Buf()]; ACCBb = [Buf(), Buf()]
        MEAN = sb("MEAN", [128, NT]); RSTD = sb("RSTD", [128, NT]); MR = sb("MR", [128, NT])
        MEANb, RSTDb, MRb = Buf(), Buf(), Buf()
        UHALO = sb("UHALO", [128, 4, 15]); RHALO = sb("RHALO", [128, 8, 3]); GHALO = sb("GHALO", [128, 2 * NFC, 2])
        UHb = [Buf() for _ in range(4)]; RHb = [Buf() for _ in range(8)]; GHb = [Buf() for _ in range(2 * NFC)]
        HST = sb("HST", [128, 8]); HSTb = [Buf() for _ in range(8)]
        HL = sb("HL", [128, 8, 5]); HLb = Buf()
        SU = sb("SU", [128, 4, 60]); SR = sb("SR", [128, 8, 12]); SG = sb("SG", [128, 2 * NFC, 8]); SH = sb("SH", [128, 8, 4])
        SSb = Buf()
        PTB = sb("PTB", [128, 1, 512], I32); PI = sb("PI", [128, 1], I32); PIF = sb("PIF", [128, 1])
        IDX = sb("IDX", [128, 512], U32); IDXb = Buf()

        ps = [es.enter_context(nc.psum_tensor("ps%d" % i, [128, 512], F32)) for i in range(8)]
        psb = [Buf(excl=True) for _ in range(8)]

        def cat(i):
            return gg[:, 8 + i, :]
        catB = ggB[8:16]

        def MM(out, lhsT, rhs, start, stop, reads, writes, signal=True):
            S.op("pe", lambda e: e.matmul(out, lhsT=lhsT, rhs=rhs, start=start, stop=stop), reads, writes, signal)

        def TR(out, in_, ident, reads, writes):
            S.op("pe", lambda e: e.transpose(out, in_, ident), reads, writes)

        def ACTF(out, in_, func, reads, writes, bias=None, scale=None):
            kw = {}
            if bias is not None:
                kw["bias"] = bias
            if scale is not None:
                kw["scale"] = scale
            S.op("act", lambda e: e.activation(out, in_, func, **kw), reads, writes)

        def CP(eng, out, in_, reads, writes):
            if eng == "act":
                S.op("act", lambda e: e.activation(out, in_, AF.Copy), reads, writes)
            else:
                S.op(eng, lambda e: e.tensor_copy(out=out, in_=in_), reads, writes)

        def TS(out, in0, s1, s2, op0, op1, reads, writes, eng="dve"):
            if op1 is None:
                S.op(eng, lambda e: e.tensor_scalar(out=out, in0=in0, scalar1=s1, scalar2=None, op0=op0), reads, writes)
            else:
                S.op(eng, lambda e: e.tensor_scalar(out=out, in0=in0, scalar1=s1, scalar2=s2, op0=op0, op1=op1), reads, writes)

        def STT(out, in0, scalar, in1, op0, op1, reads, writes):
            S.op("dve", lambda e: e.scalar_tensor_tensor(out=out, in0=in0, scalar=scalar, in1=in1, op0=op0, op1=op1), reads, writes)

        def TT(out, in0, in1, op, reads, writes, eng="dve"):
            S.op(eng, lambda e: e.tensor_tensor(out=out, in0=in0, in1=in1, op=op), reads, writes)

        def MEMSET(eng, ap, val, writes):
            S.op(eng, lambda e: e.memset(ap, val), (), writes)

        def DMA(q, out, in_, reads, writes):
            S.dma(q, lambda e: e.dma_start(out=out, in_=in_), reads, writes)

        def vcol(off):
            return VEC[:, off:off + 1]

        MEMSET("pool", onesF[:], 1.0, [CB]); MEMSET("pool", onesB[:], 1.0, [CB])
        MEMSET("pool", negones[:], -1.0, [CB]); MEMSET("pool", meanm[:], 1.0 / 1024.0, [CB])
        MEMSET("pool", zerosB[:], 0.0, [CB])

        def ASEL(out, in_, pattern, op, fill, cm):
            S.op("pool", lambda e: e.affine_select(out=out, in_=in_, pattern=pattern, compare_op=op, fill=fill,
                                                   base=0, channel_multiplier=cm), [CB], [CB])
        ASEL(identF[:], onesF[:], [[-1, 128]], ALU.is_equal, 0.0, 1)
        ASEL(identB[:], onesB[:], [[-1, 128]], ALU.is_equal, 0.0, 1)
        ASEL(negtri[:], negones[:], [[-1, 128]], ALU.is_ge, 0.0, 1)
        ASEL(maskM[:], zerosB[:], [[1, 128]], ALU.is_gt, NEG, -1)
        ASEL(maskQ8[:, 0:128], zerosB[:], [[0, 16], [1, 8]], ALU.is_gt, NEG, -1)
        ASEL(maskQ8[:, 128:256], zerosB[:], [[0, 16], [1, 8]], ALU.is_gt, NEG, -1)
        ASEL(maskQ16[:], zerosB[:], [[0, 8], [1, 16]], ALU.is_gt, NEG, -1)

        DMA("sp", VEC[:], vec, [], [VB])
        _hf, _hfb = RF.next()
        _ht, _htb = RF.next()
        HBF, HBT = _hf[0:1, 0:384], _ht[0:1, 0:384]
        DMA("sp", HBF, hbr, [], [_hfb])
        CP("dve", HBH[:], HBF, [_hfb], [VB])
        CP("dve", HBT, HBH[:], [VB], [_htb])
        TT(HBT, HBF, HBT, ALU.subtract, [_hfb, _htb], [_htb])
        CP("dve", HBL[:], HBT, [_htb], [VB])
        MEMSET("dve", QP[:], 0.0, QPB)
        ACTF(SPT[:], VEC[:, O_LAM:O_LAM + 8], AF.Exp, [VB], [VB], scale=-1.0)
        ACTF(SPT[:], SPT[:], AF.Ln, [VB], [VB], bias=1.0)
        TS(SP8[:], SPT[:], -8.0, None, ALU.mult, None, [VB], [VB])
        TS(SP4[:], SPT[:], -4.0, None, ALU.mult, None, [VB], [VB])
        S.op("pool", lambda e: e.iota(RCI[:], pattern=[[1, 16]], base=1, channel_multiplier=0), [], [VB])
        for c in range(4):
            CP("dve", RC[:, c, :], RCI[:], [VB], [VB])
            TS(RC[:, c, :], RC[:, c, :], float(2 << c), None, ALU.min, None, [VB], [VB])
            S.op("dve", (lambda cc: (lambda e: e.reciprocal(out=RC[:, cc, :], in_=RC[:, cc, :])))(c), [VB], [VB])
        DMA("pool", PW[:], poolw.rearrange("(g c) d -> c g d", c=128), [], [PWB])
        DMA("pool", LWA[:], lwa.rearrange("(n c) d -> c n d", c=128), [], [LWB])
        DMA("pool", LWX[:], lwx.rearrange("(n c) d -> c n d", c=128), [], [LWB])
        MEMSET("dve", UHALO[:], 0.0, UHb); MEMSET("dve", RHALO[:], 0.0, RHb)
        MEMSET("dve", GHALO[:], 0.0, GHb); MEMSET("dve", HST[:], 0.0, HSTb)
        DMA("sp", PTB[:], pt.rearrange("(o s) j -> o (s j)", o=1).partition_broadcast(128), [], [IDXb])
        S.op("pool", lambda e: e.iota(PI[:], pattern=[[0, 1]], base=0, channel_multiplier=1), [], [IDXb])
        CP("dve", PIF[:], PI[:], [IDXb], [IDXb])
        TS(IDX[:], PTB[:, 0, :], 128.0, PIF[:, 0:1], ALU.mult, ALU.add, [IDXb], [IDXb])

        def load_units(W, K, c0, width=512):
            return {"W": W, "K": K, "c0": c0, "width": width, "slots": {}}

        def unit_slot(u, kc):
            if kc not in u["slots"]:
                s = wr_i[0]
                wr_i[0] = (s + 1) % NW
                DMA("pool", WR[:, s, :u["width"]], u["W"][128 * kc:128 * kc + 128, u["c0"]:u["c0"] + u["width"]], [], [WRB[s]])
                u["slots"][kc] = s
            return u["slots"][kc]

        def fm_pass(units, rhs, n, banks, width=512):
            K = units["K"]
            noc = width // 128
            for kc in range(K):
                s = unit_slot(units, kc)
                rap, rb = rhs(kc)
                for oc in range(noc):
                    MM(ps[banks[oc]][:, :n], WR[:, s, 128 * oc:128 * oc + 128], rap[:, :n], kc == 0, kc == K - 1,
                       [WRB[s], rb], [psb[banks[oc]]], signal=(oc == noc - 1))

        def tm_pass(units, segs, banks, width=512):
            K = units["K"]
            for kc in range(K):
                s = unit_slot(units, kc)
                for si, (c0, m) in enumerate(segs):
                    MM(ps[banks[si]][:m, :width], xTb[:, kc, c0:c0 + m], WR[:, s, :width], kc == 0, kc == K - 1,
                       [WRB[s], xTbB[kc]], [psb[banks[si]]], signal=(si == len(segs) - 1))

        def xrhs(kc):
            return xTb[:, kc, :], xTbB[kc]

        def residual_ln(W, K, rhs, n, layer, which):
            for half in range(2):
                banks = [0, 1, 2, 3] if half == 0 else [4, 5, 6, 7]
                units = load_units(W, K, 512 * half)
                fm_pass(units, rhs, n, banks)
                for oc in range(4):
                    kc = 4 * half + oc
                    STT(xT[:, kc, :n], xT[:, kc, :n], DN_ALPHA, ps[banks[oc]][:, :n], ALU.mult, ALU.add,
                        [psb[banks[oc]], xTB[kc]], [xTB[kc]])
            for kc in range(8):
                tb, tbb = RB.next()
                CP("act", tb[:, :n], xT[:, kc, :n], [xTB[kc]], [tbb])
                sq, sqb = RB.next()
                ACTF(sq[:, :n], xT[:, kc, :n], AF.Square, [xTB[kc]], [sqb])
                MM(ps[0][:, :n], meanm[:], tb[:, :n], kc == 0, kc == 7, [CB, tbb], [psb[0]])
                MM(ps[1][:, :n], meanm[:], sq[:, :n], kc == 0, kc == 7, [CB, sqb], [psb[1]])
            CP("act", MEAN[:, :n], ps[0][:, :n], [psb[0]], [MEANb])
            TT(MR[:, :n], MEAN[:, :n], MEAN[:, :n], ALU.mult, [MEANb], [MRb])
            TT(RSTD[:, :n], ps[1][:, :n], MR[:, :n], ALU.subtract, [psb[1], MRb], [RSTDb])
            ACTF(RSTD[:, :n], RSTD[:, :n], AF.Sqrt, [RSTDb], [RSTDb], bias=LN_EPS)
            S.op("dve", lambda e: e.reciprocal(out=RSTD[:, :n], in_=RSTD[:, :n]), [RSTDb], [RSTDb])
            TT(MR[:, :n], MEAN[:, :n], RSTD[:, :n], ALU.mult, [MEANb, RSTDb], [MRb])
            gi = O_LNG + (layer * 2 + which) * 8
            bi = O_LNB + (layer * 2 + which) * 8
            for kc in range(8):
                t, tb_ = RF.next()
                TT(t[:, :n], xT[:, kc, :n], RSTD[:, :n], ALU.mult, [xTB[kc], RSTDb], [tb_])
                TT(t[:, :n], t[:, :n], MR[:, :n], ALU.subtract, [tb_, MRb], [tb_])
                ACTF(xT[:, kc, :n], t[:, :n], AF.Identity, [tb_, VB], [xTB[kc]], bias=vcol(bi + kc), scale=vcol(gi + kc))
                CP("act", xTb[:, kc, :n], xT[:, kc, :n], [xTB[kc]], [xTbB[kc]])

        def seg_layout(segs, H):
            out = []
            for si, (c0, m, kind) in enumerate(segs):
                base = 0 if si == 0 else (H + 16) + (si - 1) * (H + 8)
                out.append((base, c0, m, kind))
            tot = out[-1][0] + H + out[-1][2]
            return out, tot

        def halo_evac(bank, bankb, segs, H, halo_ap, halo_b, samp_ap, update=True):
            lay, tot = seg_layout(segs, H)
            w, wb = RF.next()
            for si, (base, c0, m, kind) in enumerate(lay):
                if kind == "p":
                    CP("dve", w[:, base:base + H], halo_ap, [halo_b], [wb])
                else:
                    CP("dve", w[:, base:base + H], samp_ap(int(kind[1])), [SSb], [wb])
                CP("act", w[:, base + H:base + H + m], bank[:, c0:c0 + m], [bankb], [wb])
            if update and len(segs) == 1:
                m = segs[0][1]
                CP("dve", halo_ap, w[:, m:m + H], [wb], [halo_b])
            return w, wb, lay, tot

        def ffn(layer, n, segs, tail):
            W = fup[layer]
            for cg in range(11):
                banks = [0, 1, 2, 3] if cg % 2 == 0 else [4, 5, 6, 7]
                units = load_units(W, 8, 512 * cg)
                fm_pass(units, xrhs, n, banks)
                ev = {}
                for i in range(4):
                    ch = 4 * cg + i
                    if ch < NFC:
                        hi = layer * NFC + ch
                        ev[i] = halo_evac(ps[banks[i]], psb[banks[i]], segs, 2, GHALO[:, hi, :], GHb[hi],
                                          lambda s, hi=hi: SG[:, hi, 2 * s:2 * s + 2])
                for i in range(4):
                    ch = 4 * cg + i
                    bk, bkb = ps[banks[i]], psb[banks[i]]
                    if ch < NFC:
                        w, wb, lay, tot = ev[i]
                        wo = O_FCW + layer * 66 + ch
                        t, tb_ = RF.next()
                        TS(t[:, 2:tot], w[:, 2:tot], vcol(wo + 44), vcol(O_FCB + layer * NFC + ch), ALU.mult, ALU.add,
                           [wb, VB], [tb_])
                        STT(t[:, 2:tot], w[:, 1:tot - 1], vcol(wo + 22), t[:, 2:tot], ALU.mult, ALU.add, [wb, tb_, VB], [tb_])
                        g2, g2b = RF.next()
                        for (base, c0, m, kind) in lay:
                            STT(g2[:, c0:c0 + m], w[:, base:base + m], vcol(wo), t[:, base + 2:base + 2 + m],
                                ALU.mult, ALU.add, [wb, tb_, VB], [g2b])
                        ACTF(gg[:, ch, :n], g2[:, :n], AF.Gelu, [g2b], [ggB[ch]])
                    else:
                        c2 = ch - NFC
                        TT(gg[:, c2, :n], gg[:, c2, :n], bk[:, :n], ALU.mult, [ggB[c2], bkb], [ggB[c2]])
                if tail and cg < 6:
                    tsegs = [(c0, m) for (c0, m, k) in segs]
                    tbk = [4, 5, 6, 7, 3] if cg % 2 == 0 else [0, 1, 2, 3, 7]
                    width = 512 if cg < 5 else 256
                    tm_pass(units, tsegs, tbk, width)
                    for si, (c0, m) in enumerate(tsegs):
                        st, stb = RF.next()
                        CP("act", st[:m, :width], ps[tbk[si]][:m, :width], [psb[tbk[si]]], [stb])
                        if si == 0:
                            DMA("sp", fcp[layer, :, 512 * cg:512 * cg + width], st[m - 2:m, :width], [stb], [])
                        else:
                            DMA("sp", fcs[layer, 2 * (si - 1):2 * si, 512 * cg:512 * cg + width], st[m - 2:m, :width], [stb], [])
            residual_ln(fdn[layer], NFC, lambda kc: (gg[:, kc, :], ggB[kc]), n, layer, 1)

        def attn_main(g, n):
            jmax = 4 * g + 3
            zr = [0]
            for h in range(8):
                p, r0 = h // 2, 64 * (h % 2)
                ob = 4 + (h % 2)
                a_, ab_ = h % 2, h % 2
                MM(ps[ob][:, :n], zerosB[:], QP[:, h, :n], True, False, [CB, QPB[h]], [psb[ob]])
                MEMSET("dve", ACC[:, a_, :], 0.0, [ACCb[a_]])
                hbias = vcol(O_HB + h)
                st = {}

                def stage1a(j):
                    z = zr[0]
                    zr[0] = (z + 1) % 4
                    c0 = max(0, 128 * (j - 4 * g))
                    diag = j >= 4 * g
                    MM(ps[z][:, c0:n], KT[:, p, 128 * j:128 * j + 128], QP[:, h, c0:n], True, False,
                       [KTB[p], QPB[h]], [psb[z]], signal=not diag)
                    if diag:
                        MM(ps[z][:, c0:c0 + 128], identB[:], maskM[:], False, False, [CB], [psb[z]])
                    st[j] = [z, c0, diag, None, None]

                def stage1b(j):
                    z, c0, diag, _, _ = st[j]
                    E, Eb = RF.next()
                    ACTF(E[:, c0:n], ps[z][:, c0:n], AF.Exp, [psb[z], VB], [Eb], bias=hbias)
                    Lp, Lpb = RB.next()
                    ACTF(Lp[:, c0:n], E[:, c0:n], AF.Ln, [Eb], [Lpb], bias=1.0)
                    st[j][3], st[j][4] = Lp, Lpb
                    if j > 0:
                        TT(ACC[:, a_, c0:n], ACC[:, a_, c0:n], Lp[:, c0:n], ALU.add, [ACCb[a_], Lpb], [ACCb[a_]])

                def stage2a(j):
                    z, c0, diag, Lp, Lpb = st[j]
                    cc0 = c0 + 128 if diag else 0
                    carry = (j != jmax) and cc0 < n
                    MM(ps[z][:, c0:n], negtri[:], Lp[:, c0:n], False, not carry, [CB, Lpb], [psb[z]], signal=not carry)
                    if carry:
                        MM(ps[z][:, cc0:n], negones[:], ACCB[:, ab_, cc0:n], False, True, [CB, ACCBb[ab_]], [psb[z]])
                    if j > 0:
                        CP("dve", ACCB[:, ab_, c0:n], ACC[:, a_, c0:n], [ACCb[a_]], [ACCBb[ab_]])

                def stage2b(j):
                    z, c0, diag, Lp, Lpb = st[j]
                    A, Ab = RB.next()
                    ACTF(A[:, c0:n], ps[z][:, c0:n], AF.Exp, [psb[z], VB], [Ab], bias=hbias)
                    st[j] += [A, Ab]

                def stage2c(j):
                    z, c0, diag, Lp, Lpb, A, Ab = st.pop(j)
                    MM(ps[ob][:, c0:n], V[:, j, 128 * p:128 * p + 128], A[:, c0:n], False, j == 0, [VBl[j], Ab], [psb[ob]])

                stage1a(jmax)
                if jmax >= 1:
                    stage1a(jmax - 1)
                stage1b(jmax)
                for j in range(jmax, -1, -1):
                    stage2a(j)
                    if j < jmax:
                        stage2c(j + 1)
                    if j >= 2:
                        stage1a(j - 2)
                    if j >= 1:
                        stage1b(j - 1)
                    stage2b(j)
                stage2c(0)
                CP("dve", cat(p)[r0:r0 + 64, :n], ps[ob][r0:r0 + 64, :n], [psb[ob]], [catB[p]])

        def attn_batched(items, nq, nblocks, get_block, maskQ, hb0):
            ni = len(items)
            Wd = ni * 8 * nq
            ob = 3
            MM(ps[ob][:, :Wd], zerosB[:], maskQ[:, :Wd], True, False, [CB], [psb[ob]])
            MEMSET("dve", ACC[:, 0, :], 0.0, [ACCb[0]])
            zr = [0]
            st = {}

            def stage1(j):
                blk, nk, diag = get_block(j)
                z = zr[0]
                zr[0] = (z + 1) % 3
                first = True
                for it, (qc0, oc0) in enumerate(items):
                    kt, ktb, v_ap, vb = blk[it]
                    for p in range(4):
                        col = (it * 8 + 2 * p) * nq
                        MM(ps[z][:nk, col:col + 2 * nq].rearrange("k (h q) -> k h q", h=2), kt(p)[:, :nk],
                           QP[:, 2 * p:2 * p + 2, qc0:qc0 + nq], first, False,
                           list(ktb) + [QPB[2 * p], QPB[2 * p + 1]], [psb[z]], signal=False)
                        first = False
                MM(ps[z][:nk, :Wd], onesB[0:1, :nk], HBH[0:1, hb0:hb0 + Wd], False, False, [CB, VB], [psb[z]], signal=False)
                MM(ps[z][:nk, :Wd], onesB[0:1, :nk], HBL[0:1, hb0:hb0 + Wd], False, False, [CB, VB], [psb[z]], signal=not diag)
                if diag:
                    MM(ps[z][:nk, :Wd], identB[:nk, :nk], maskQ[:nk, :Wd], False, False, [CB], [psb[z]])
                E, Eb = RF.next()
                ACTF(E[:nk, :Wd], ps[z][:nk, :Wd], AF.Exp, [psb[z]], [Eb])
                Lp, Lpb = RB.next()
                ACTF(Lp[:nk, :Wd], E[:nk, :Wd], AF.Ln, [Eb], [Lpb], bias=1.0)
                st[j] = (z, blk, nk, Lp, Lpb)

            def stage2(j):
                z, blk, nk, Lp, Lpb = st.pop(j)
                carry = j != nblocks - 1
                MM(ps[z][:nk, :Wd], negtri[:nk, :nk], Lp[:nk, :Wd], False, not carry, [CB, Lpb], [psb[z]], signal=not carry)
                if carry:
                    MM(ps[z][:nk, :Wd], negones[:, :nk], ACCB[:, 0, :Wd], False, True, [CB, ACCBb[0]], [psb[z]])
                A, Ab = RB.next()
                ACTF(A[:nk, :Wd], ps[z][:nk, :Wd], AF.Exp, [psb[z]], [Ab])
                cnt = 0
                for it in range(ni):
                    kt, ktb, v_ap, vb = blk[it]
                    for p in range(4):
                        col = (it * 8 + 2 * p) * nq
                        cnt += 1
                        MM(ps[ob][:, col:col + 2 * nq], v_ap[:nk, 128 * p:128 * p + 128], A[:nk, col:col + 2 * nq], False,
                           (j == 0 and cnt == ni * 4), list(vb) + [Ab], [psb[ob]], signal=(cnt == ni * 4))
                if j > 0:
                    TT(ACC[:nk, 0, :Wd], ACC[:nk, 0, :Wd], Lp[:nk, :Wd], ALU.add, [ACCb[0], Lpb], [ACCb[0]])
                    CP("dve", ACCB[:, 0, :Wd], ACC[:, 0, :Wd], [ACCb[0]], [ACCBb[0]])

            stage1(nblocks - 1)
            for j in range(nblocks - 1, -1, -1):
                if j > 0:
                    stage1(j - 1)
                stage2(j)
            for it, (qc0, oc0) in enumerate(items):
                for h in range(8):
                    p, r0 = h // 2, 64 * (h % 2)
                    col = (it * 8 + h) * nq
                    CP("dve", cat(p)[r0:r0 + 64, oc0:oc0 + nq], ps[ob][r0:r0 + 64, col:col + nq], [psb[ob]], [catB[p]])

        def load_sample_states():
            def tr_rows(src, rows, nchunks, dst_fn):
                w = nchunks * 128
                view = xin[:rows, :, :].rearrange("p a b -> p (a b)")
                DMA("sp", view[:, :w], src, [], xinb)
                for c in range(nchunks):
                    bk = c % 8
                    TR(ps[bk][:, :rows], view[:, 128 * c:128 * c + 128], identF[:rows, :rows], xinb + [CB], [psb[bk]])
                    CP("dve", dst_fn(c), ps[bk][:, :rows], [psb[bk]], [SSb])
            tr_rows(spool, 60, 4, lambda c: SU[:, c, :])
            tr_rows(slc, 12, 8, lambda c: SR[:, c, :])
            tr_rows(slh, 4, 8, lambda c: SH[:, c, :])
            for l in range(2):
                tr_rows(sfc[l], 8, NFC, lambda c, l=l: SG[:, l * NFC + c, :])
            for s in range(4):
                t, tb_ = RF.next()
                DMA("sp", t[:7, :512], spool[15 * s + 8:15 * s + 15, :], [], [tb_])
                DMA("sp", pools[15 * s:15 * s + 7, :], t[:7, :512], [tb_], [])

        groups = [(512, False, 512 * g) for g in range(NG)] + [(48, True, 4096)]
        S.mark('consts')
        load_sample_states()
        S.mark('states')
        for gi_, (n, tail, pos0) in enumerate(groups):
            if not tail:
                segs = [(0, 512, "p")]
                tsegs = [(128 * i, 128) for i in range(4)]
            else:
                segs = [(0, 16, "p")] + [(16 + 8 * s, 8, "s%d" % s) for s in range(4)]
                tsegs = [(c0, m) for (c0, m, k) in segs]
            npr = 512 if not tail else 16

            xv = xin[:, :, :].rearrange("p a b -> p (a b)")
            if not tail:
                for i in range(4):
                    rows0 = pos0 + 128 * i
                    dstv = xin[:, 2 * i:2 * i + 2, :].rearrange("p a b -> p (a b)")
                    if rows0 == 0:
                        DMA("sp", dstv[0:16, :], meta, [], xinb[0:2])
                        DMA("sp", dstv[16:128, :], xp[0:112, :], [], xinb[0:2])
                    else:
                        DMA("sp", dstv[:, :], xp[rows0 - 16:rows0 + 112, :], [], xinb[2 * i:2 * i + 2])
                for kc in range(8):
                    for i in range(4):
                        dstv = xin[:, 2 * i:2 * i + 2, :].rearrange("p a b -> p (a b)")
                        TR(ps[kc][:, 128 * i:128 * i + 128], dstv[:, 128 * kc:128 * kc + 128], identF[:], xinb[2 * i:2 * i + 2] + [CB], [psb[kc]])
            else:
                dstv = xin[:, 0:2, :].rearrange("p a b -> p (a b)")
                DMA("sp", dstv[0:16, :], xp[SEQ - 16:SEQ, :], [], xinb[0:2])
                DMA("sp", dstv[16:48, :], xs, [], xinb[0:2])
                for kc in range(8):
                    TR(ps[kc][:, 0:48], dstv[0:48, 128 * kc:128 * kc + 128], identF[:48, :48], xinb[0:2] + [CB], [psb[kc]])
            for kc in range(8):
                CP("act", xT[:, kc, :n], ps[kc][:, :n], [psb[kc]], [xTB[kc]])
                CP("dve", xTb[:, kc, :n], ps[kc][:, :n], [psb[kc]], [xTbB[kc]])

            S.mark('g%d:G0' % gi_)
            units = load_units(w_in0, 8, 0)
            fm_pass(units, xrhs, n, [0, 1, 2, 3])
            for h in range(8):
                r0 = 64 * (h % 2)
                ACTF(QP[r0:r0 + 64, h, :n], ps[h // 2][r0:r0 + 64, :n], AF.Copy, [psb[h // 2]], [QPB[h]], scale=0.125)
            units = load_units(w_in0, 8, 512)
            fm_pass(units, xrhs, n, [4, 5, 6, 7])
            for oc in range(4):
                CP("dve", KT[:, oc, pos0:pos0 + npr], ps[4 + oc][:, :npr], [psb[4 + oc]], [KTB[oc]])
                if tail:
                    CP("dve", KTS[:, oc, :], ps[4 + oc][:, 16:48], [psb[4 + oc]], [KTSB])
            kb = [0, 1, 2, 3] if not tail else [0, 1, 2, 3, 4]
            tm_pass(units, tsegs, kb)
            for si, (c0, m) in enumerate(tsegs):
                st, stb = RF.next()
                CP("act", st[:m, :512], ps[kb[si]][:m, :], [psb[kb[si]]], [stb])
                if not tail or si == 0:
                    DMA("sp", kp[pos0 + c0:pos0 + c0 + m, :], st[:m, :512], [stb], [])
                else:
                    DMA("sp", ks[8 * (si - 1):8 * si, :], st[:m, :512], [stb], [])
            units = load_units(w_in0, 8, 1024)
            vb_ = [4, 5, 6, 7] if not tail else [5, 6, 7, 3, 4]
            tm_pass(units, tsegs, vb_)
            for si, (c0, m) in enumerate(tsegs):
                st, stb = RF.next()
                CP("act", st[:m, :512], ps[vb_[si]][:m, :], [psb[vb_[si]]], [stb])
                if not tail or si == 0:
                    blk = (pos0 + c0) // 128
                    CP("dve", V[:m, blk, :], ps[vb_[si]][:m, :], [psb[vb_[si]]], [VBl[blk]])
                    DMA("sp", vp[pos0 + c0:pos0 + c0 + m, :], st[:m, :512], [stb], [])
                else:
                    CP("dve", VSN[:, si - 1, :], ps[vb_[si]][:m, :], [psb[vb_[si]]], [VSNB[si - 1]])
                    DMA("sp", vs[8 * (si - 1):8 * si, :], st[:m, :512], [stb], [])
            units = load_units(w_in0, 8, 1536)
            fm_pass(units, xrhs, n, [0, 1, 2, 3])
            for c in range(4):
                w, wb, lay, tot = halo_evac(ps[c], psb[c], segs, 15, UHALO[:, c, :], UHb[c],
                                            lambda s, c=c: SU[:, c, 15 * s:15 * s + 15])
                wn = 2 << c
                a, ab = RF.next()
                TT(a[:, 1:tot], w[:, 1:tot], w[:, 0:tot - 1], ALU.add, [wb], [ab])
                cur, curb, sh = a, ab, 2
                while sh < wn:
                    nb, nbb = RF.next()
                    lo = 2 * sh - 1
                    TT(nb[:, lo:tot], cur[:, lo:tot], cur[:, lo - sh:tot - sh], ALU.add, [curb], [nbb])
                    cur, curb, sh = nb, nbb, 2 * sh
                dd, ddb = RB.next()
                for (base, c0, m, kind) in lay:
                    o = base + 15
                    STT(dd[:, c0:c0 + m], cur[:, o:o + m], 1.0 / wn, w[:, o:o + m], ALU.mult, ALU.subtract, [curb, wb], [ddb])
                    if gi_ == 0:
                        t2, t2b = RF.next()
                        TT(t2[:, 0:16], cur[:, o:o + 16], RC[:, c, :], ALU.mult, [curb, VB], [t2b])
                        TT(dd[:, c0:c0 + 16], t2[:, 0:16], w[:, o:o + 16], ALU.subtract, [t2b, wb], [ddb])
                bk = 6 + (c % 2)
                MM(ps[bk][:, :n], PW[:, c, :], dd[:, :n], True, True, [PWB, ddb], [psb[bk]])
                TS(cat(4 + c)[:, :n], ps[bk][:, :n], vcol(O_PSC + c), None, ALU.mult, None, [psb[bk], VB], [catB[4 + c]])
            if tail:
                ub = [4, 5, 6, 7, 0]
                tm_pass(units, tsegs, ub)
                for si, (c0, m) in enumerate(tsegs):
                    st, stb = RF.next()
                    CP("act", st[:m, :512], ps[ub[si]][:m, :], [psb[ub[si]]], [stb])
                    if si == 0:
                        DMA("sp", poolp[:, :], st[1:16, :512], [stb], [])
                    else:
                        DMA("sp", pools[15 * (si - 1) + 7:15 * si, :], st[:8, :512], [stb], [])
            S.mark('g%d:inproj+pool' % gi_)
            if not tail:
                attn_main(gi_, n)
            else:
                def blk_prompt(j):
                    nk = 128 if j < 32 else 16
                    return ([(lambda p, j=j, nk=nk: KT[:, p, 128 * j:128 * j + nk], KTB, V[:, j, :], [VBl[j]])], nk, j == 32)
                attn_batched([(0, 0)], 16, NBLK, blk_prompt, maskQ16, 256)

                pg_state = {}

                def blk_sample(j):
                    if j == PAGES:
                        return ([(lambda p, s=s: KTS[:, p, 8 * s:8 * s + 8], [KTSB], VSN[:, s, :], [VSNB[s]]) for s in range(4)], 8, True)
                    res = []
                    for s in range(4):
                        col = s * 128 + j
                        kpg, kpb = xin[:, 2 * s, :], xinb[2 * s]
                        vpg, vpb = xin[:, 2 * s + 1, :], xinb[2 * s + 1]
                        S.dma("pool", (lambda kpg=kpg, col=col: (lambda e: e.indirect_dma_start(
                            out=kpg, out_offset=None, in_=ck, in_offset=bass.IndirectOffsetOnAxis(ap=IDX[:, col:col + 1], axis=0))))(),
                            [IDXb], [kpb])
                        S.dma("pool", (lambda vpg=vpg, col=col: (lambda e: e.indirect_dma_start(
                            out=vpg, out_offset=None, in_=cv, in_offset=bass.IndirectOffsetOnAxis(ap=IDX[:, col:col + 1], axis=0))))(),
                            [IDXb], [vpb])
                        tb = 4 + s
                        for p in range(4):
                            TR(ps[tb][:, 128 * p:128 * p + 128], kpg[:, 128 * p:128 * p + 128], identF[:], [kpb, CB], [psb[tb]])
                        vch = (4 + s) if (j % 2 == 0) else (16 + s)
                        CP("dve", gg[:, s, :], ps[tb][:, :], [psb[tb]], [ggB[s]])
                        CP("dve", gg[:, vch, :], vpg, [vpb], [ggB[vch]])
                        res.append((lambda p, s=s: gg[:, s, 128 * p:128 * p + 128], [ggB[s]], gg[:, vch, :], [ggB[vch]]))
                    return (res, 128, False)
                attn_batched([(16 + 8 * s, 16 + 8 * s) for s in range(4)], 8, PAGES + 1, blk_sample, maskQ8, 0)
            S.mark('g%d:attn' % gi_)
            residual_ln(w_out0, 8, lambda kc: (cat(kc), catB[kc]), n, 0, 0)
            S.mark('g%d:ln1' % gi_)
            ffn(0, n, segs, tail)
            S.mark('g%d:ffn0' % gi_)

            for cg in range(2):
                banks = [0, 1, 2, 3] if cg == 0 else [4, 5, 6, 7]
                units = load_units(lw_in, 8, 512 * cg)
                fm_pass(units, xrhs, n, banks)
                for i in range(4):
                    c = 4 * cg + i
                    ACTF(gg[:, c, :n], ps[banks[i]][:, :n], AF.Gelu, [psb[banks[i]]], [ggB[c]])
            for cg in range(2, 4):
                banks = [0, 1, 2, 3] if cg == 2 else [4, 5, 6, 7]
                units = load_units(lw_in, 8, 512 * cg)
                fm_pass(units, xrhs, n, banks)
                if tail:
                    tbk = [4, 5, 6, 7, 3] if cg == 2 else [0, 1, 2, 3, 7]
                    tm_hold = (units, tbk, cg)
                else:
                    tm_hold = None
                for i in range(4):
                    c = 4 * (cg - 2) + i
                    bk, bkb = ps[banks[i]], psb[banks[i]]
                    w, wb, lay, tot = halo_evac(bk, bkb, segs, 3, RHALO[:, c, :], RHb[c], lambda s, c=c: SR[:, c, 3 * s:3 * s + 3])
                    t, tb_ = RF.next()
                    TS(t[:, 3:tot], w[:, 3:tot], vcol(O_LCW + 24 + c), vcol(O_LCB + c), ALU.mult, ALU.add, [wb, VB], [tb_])
                    STT(t[:, 3:tot], w[:, 2:tot - 1], vcol(O_LCW + 16 + c), t[:, 3:tot], ALU.mult, ALU.add, [wb, tb_, VB], [tb_])
                    STT(t[:, 3:tot], w[:, 1:tot - 2], vcol(O_LCW + 8 + c), t[:, 3:tot], ALU.mult, ALU.add, [wb, tb_, VB], [tb_])
                    xc, xcb = RF.next()
                    for (base, c0, m, kind) in lay:
                        STT(xc[:, c0:c0 + m], w[:, base:base + m], vcol(O_LCW + c), t[:, base + 3:base + 3 + m],
                            ALU.mult, ALU.add, [wb, tb_, VB], [xcb])
                    xb16, xb16b = RB.next()
                    CP("act", xb16[:, :n], xc[:, :n], [xcb], [xb16b])
                    bnk = banks[i]
                    r, rb = RF.next()
                    MM(ps[bnk][:, :n], LWA[:, c, :], xb16[:, :n], True, True, [LWB, xb16b], [psb[bnk]])
                    ACTF(r[:, :n], ps[bnk][:, :n], AF.Sigmoid, [psb[bnk], VB], [rb], bias=vcol(O_LBA + c))
                    gi2, gib = RF.next()
                    MM(ps[bnk][:, :n], LWX[:, c, :], xb16[:, :n], True, True, [LWB, xb16b], [psb[bnk]])
                    ACTF(gi2[:, :n], ps[bnk][:, :n], AF.Sigmoid, [psb[bnk], VB], [gib], bias=vcol(O_LBX + c))
                    a, ab = RF.next()
                    ACTF(a[:, :n], r[:, :n], AF.Tanh, [rb, VB], [ab], scale=SP4[:, c:c + 1])
                    t1, t1b = RF.next()
                    TS(t1[:, :n], a[:, :n], -1.0, 1.0, ALU.mult, ALU.add, [ab], [t1b])
                    S.op("dve", (lambda t1=t1, n=n: (lambda e: e.reciprocal(out=t1[:, :n], in_=t1[:, :n])))(), [t1b], [t1b])
                    TS(a[:, :n], a[:, :n], 1.0, None, ALU.add, None, [ab], [ab])
                    TT(a[:, :n], a[:, :n], t1[:, :n], ALU.mult, [ab, t1b], [ab])
                    ACTF(r[:, :n], r[:, :n], AF.Tanh, [rb, VB], [rb], scale=SP8[:, c:c + 1])
                    TT(t1[:, :n], a[:, :n], a[:, :n], ALU.mult, [ab, t1b], [t1b])
                    STT(t1[:, :n], t1[:, :n], 1.0, r[:, :n], ALU.add, ALU.mult, [t1b, rb], [t1b])
                    ACTF(t1[:, :n], t1[:, :n], AF.Sqrt, [t1b], [t1b], scale=-1.0)
                    TT(gi2[:, :n], gi2[:, :n], xc[:, :n], ALU.mult, [gib, xcb], [gib])
                    TT(gi2[:, :n], gi2[:, :n], t1[:, :n], ALU.mult, [gib, t1b], [gib])
                    hh, hb_ = RF.next()
                    for si, (base, c0, m, kind) in enumerate(lay):
                        if kind == "p":
                            init, initb = HST[:, c:c + 1], HSTb[c]
                        else:
                            s_ = int(kind[1])
                            init, initb = SH[:, c, s_:s_ + 1], SSb
                        S.op("dve", (lambda c0=c0, m=m, init=init, hh=hh, a=a, gi2=gi2: (lambda e: e.tensor_tensor_scan(
                            out=hh[:, c0:c0 + m], data0=a[:, c0:c0 + m], data1=gi2[:, c0:c0 + m], initial=init,
                            op0=ALU.mult, op1=ALU.add)))(), [ab, gib, initb], [hb_])
                        if kind == "p":
                            CP("dve", HST[:, c:c + 1], hh[:, c0 + m - 1:c0 + m], [hb_], [HSTb[c]])
                        if tail:
                            CP("dve", HL[:, c, si:si + 1], hh[:, c0 + m - 1:c0 + m], [hb_], [HLb])
                    TT(cat(c)[:, :n], gg[:, c, :n], hh[:, :n], ALU.mult, [ggB[c], hb_], [catB[c]])
                if tm_hold is not None:
                    units_, tbk, cg_ = tm_hold
                    tm_pass(units_, tsegs, tbk)
                    for si, (c0, m) in enumerate(tsegs):
                        st, stb = RF.next()
                        CP("act", st[:m, :512], ps[tbk[si]][:m, :], [psb[tbk[si]]], [stb])
                        cc = 512 * (cg_ - 2)
                        if si == 0:
                            DMA("sp", lcp[:, cc:cc + 512], st[m - 3:m, :512], [stb], [])
                        else:
                            DMA("sp", lcs[3 * (si - 1):3 * si, cc:cc + 512], st[m - 3:m, :512], [stb], [])
            if tail:
                for c in range(8):
                    bk = 0 if c < 4 else 1
                    TR(ps[bk][:5, 128 * (c % 4):128 * (c % 4) + 128], HL[:, c, :], identF[:], [HLb, CB], [psb[bk]])
                for half in range(2):
                    st, stb = RF.next()
                    CP("act", st[:5, :512], ps[half][:5, :], [psb[half]], [stb])
                    DMA("sp", lhp[:, 512 * half:512 * half + 512], st[0:1, :512], [stb], [])
                    DMA("sp", lhs[:, 512 * half:512 * half + 512], st[1:5, :512], [stb], [])
            S.mark('g%d:lru' % gi_)
            residual_ln(lw_out, 8, lambda kc: (cat(kc), catB[kc]), n, 1, 0)
            S.mark('g%d:ln3' % gi_)
            ffn(1, n, segs, tail)
            S.mark('g%d:ffn1' % gi_)

            for si, (c0, m) in enumerate(tsegs if not tail else [(0, 48)]):
                b0 = 2 * (si % 4)
                for kc in range(8):
                    bk = b0 + (kc // 4)
                    TR(ps[bk][:m, 128 * (kc % 4):128 * (kc % 4) + 128], xT[:, kc, c0:c0 + m], identF[:], [xTB[kc], CB], [psb[bk]])
                for half in range(2):
                    st, stb = RF.next()
                    CP("act" if half == 0 else "dve", st[:m, :512], ps[b0 + half][:m, :], [psb[b0 + half]], [stb])
                    cs = slice(512 * half, 512 * half + 512)
                    if not tail:
                        r0_ = pos0 + c0 - 16
                        if r0_ < 0:
                            DMA("sp", yp[0:112, cs], st[16:128, :512], [stb], [])
                        else:
                            DMA("sp", yp[r0_:r0_ + 128, cs], st[:128, :512], [stb], [])
                    else:
                        DMA("sp", yp[SEQ - 16:SEQ, cs], st[0:16, :512], [stb], [])
                        DMA("sp", ys[:, cs], st[16:48, :512], [stb], [])

        S.finish()
        build_program.stats = {e: len(S.prog[e]) for e in ENGS}
        build_program.marks = S.marks
        with nc.Block() as block:
            S.emit(block)
    return nc


_NC_CACHE = {}


def kernel(**inp):
    f = lambda a: np.ascontiguousarray(np.asarray(a, dtype=np.float32))
    x_prompt = f(inp["x_prompt"]); x_sample = f(inp["x_sample"])
    ck = f(inp["cache_sb_k"]).reshape(NPHYS * 128, 512)
    cv = f(inp["cache_sb_v"]).reshape(NPHYS * 128, 512)
    page_table = np.ascontiguousarray(np.asarray(inp["page_table"], dtype=np.int32))

    def cm(v):
        return np.ascontiguousarray(f(v).reshape(-1, 128).T)
    hb = f(inp["sb_logit_bias"])[0]
    vec = np.concatenate([
        cm(inp["ln_g"]), cm(inp["ln_b"]), cm(inp["pool_scale"]), cm(inp["lru_conv_w"]), cm(inp["lru_conv_b"]),
        cm(inp["lru_b_a"]), cm(inp["lru_b_x"]), cm(inp["lru_lambda"]), cm(inp["ffn_conv_w"]), cm(inp["ffn_conv_b"]),
        np.tile(hb[None, :], (128, 1))], axis=1).astype(np.float32)
    assert vec.shape == (128, VR)
    hbr = np.concatenate([np.tile(np.repeat(hb, 8), 4), np.repeat(hb, 16)])[None, :].astype(np.float32)

    common = dict(
        meta=f(inp["meta_tokens"]), ck=ck, cv=cv,
        w_in0=f(inp["sb_w_in"])[0], w_out0=f(inp["sb_w_out"])[0], poolw=f(inp["pool_w"])[0].reshape(512, 128),
        lw_in=f(inp["lru_w_in"])[0], lwa=f(inp["lru_w_a"])[0].reshape(1024, 128), lwx=f(inp["lru_w_x"])[0].reshape(1024, 128),
        lw_out=f(inp["lru_w_out"])[0], fup0=f(inp["ffn_w_up"])[0], fup1=f(inp["ffn_w_up"])[1],
        fdn0=f(inp["ffn_w_down"])[0], fdn1=f(inp["ffn_w_down"])[1], vec=vec, hbr=hbr)
    in_maps = []
    for c in range(8):
        b = c % 4
        sl = slice(4 * c, 4 * c + 4)
        m = dict(common)
        m.update(
            xp=x_prompt[b], xs=np.ascontiguousarray(x_sample[sl].reshape(32, D)),
            pt=np.ascontiguousarray(page_table[sl]),
            spool=np.ascontiguousarray(f(inp["state_pool"])[0, sl].reshape(60, 512)),
            slc=np.ascontiguousarray(f(inp["state_lru_conv"])[0, sl].reshape(12, D)),
            slh=np.ascontiguousarray(f(inp["state_lru_h"])[0, sl]),
            sfc=np.ascontiguousarray(f(inp["state_ffn_conv"])[:, sl].reshape(2, 8, FFN)))
        in_maps.append(m)
    if "nc" not in _NC_CACHE:
        _NC_CACHE["nc"] = build_program()
    res = run_bass_kernel_spmd(_NC_CACHE["nc"], in_maps, core_ids=list(range(8)))
    R = res.results
    pr = lambda k: np.stack([R[b][k] for b in range(4)])
    sm = lambda k: np.concatenate([R[c][k] for c in range(8)], axis=0)
    y_prompt = pr("yp")
    y_sample = sm("ys").reshape(32, 8, D)
    k_prompt = pr("kp").reshape(1, 4, L, 8, 64); v_prompt = pr("vp").reshape(1, 4, L, 8, 64)
    k_sample = sm("ks").reshape(1, 32, 8, 8, 64); v_sample = sm("vs").reshape(1, 32, 8, 8, 64)
    pool_prompt = pr("poolp").reshape(1, 4, 15, 512); pool_sample = sm("pools").reshape(1, 32, 15, 512)
    lcp = pr("lcp").reshape(1, 4, 3, D); lcs = sm("lcs").reshape(1, 32, 3, D)
    lhp = pr("lhp").reshape(1, 4, D); lhs = sm("lhs").reshape(1, 32, D)
    fcp = np.stack([R[b]["fcp"] for b in range(4)], axis=1)
    fcs = np.concatenate([R[c]["fcs"].reshape(2, 4, 2, FFN) for c in range(8)], axis=1)
    outs = (y_prompt, y_sample, k_prompt, v_prompt, k_sample, v_sample, pool_prompt, pool_sample,
            lcp, lcs, lhp, lhs, fcp, fcs)
    return tuple(np.ascontiguousarray(o, dtype=np.float32) for o in outs)
```
